# Optimizing a Trainium2 kernel written in Bass

```python
import jax, jax.numpy as jnp
from jax import lax
import numpy as np

D_MODEL = 2048
BATCH = 2
SEQ = 4096
DEPTH = 1

GLA_HEADS = 4
GLA_VAL = D_MODEL // 2
GLA_DV = GLA_VAL // GLA_HEADS
GLA_DK = GLA_DV // 2
GLA_KEY = GLA_HEADS * GLA_DK
GLA_LR = 16
GLA_TAU = 16.0
GLA_CHUNK = 64
RW_HEAD = 64
RW_WIDTH = D_MODEL // 2
RW_HEADS = RW_WIDTH // RW_HEAD
RW_W_LR = 64
RW_A_LR = 64
RW_G_LR = 128
RW_DECAY_SCALE = 0.606531
RW_LN_EPS = 64e-5
N_EXPERTS = 16
CAPACITY = 2
D_FF_EXPERT = D_MODEL // 2
DEEPNORM_ALPHA = (2 * DEPTH) ** 0.25
DEEPNORM_BETA = (8 * DEPTH) ** -0.25
LN_EPS = 1e-5

GLA_SPLITS = (GLA_KEY, GLA_KEY, GLA_VAL, GLA_VAL, GLA_LR, GLA_LR)
RW_SPLITS = (RW_WIDTH, RW_WIDTH, RW_WIDTH, RW_W_LR, RW_W_LR, RW_A_LR, RW_G_LR)
GLA_COLS = sum(GLA_SPLITS)
RW_COLS = sum(RW_SPLITS)
N_IN_COLS = GLA_COLS + RW_COLS + 2 * D_MODEL

kernel_name = 'hybrid_gla_rwkv7_ecmoe_deepnorm'


def _split(t, sizes):
    return jnp.split(t, np.cumsum(sizes)[:-1].tolist(), axis=-1)


def _rev(t):
    return jnp.flip(t, axis=1)


def layer_norm(x, g, b):
    xf = x.astype(jnp.float32)
    mu = jnp.mean(xf, -1, keepdims=True)
    var = jnp.mean(jnp.square(xf - mu), -1, keepdims=True)
    return ((xf - mu) * lax.rsqrt(var + LN_EPS) * g.astype(jnp.float32) + b.astype(jnp.float32)).astype(x.dtype)


def gla_chunked(q, k, v, log_a):
    B, S, H, K = q.shape
    V = v.shape[-1]
    C = GLA_CHUNK
    n = S // C

    def blocks(t):
        return t.astype(jnp.float32).reshape(B, n, C, H, t.shape[-1]).transpose(1, 0, 3, 2, 4)

    tri = jnp.tril(jnp.ones((C, C), dtype=bool))

    def step(state, inp):
        qc, kc, vc, gc = inp
        b = jnp.cumsum(gc, axis=2)
        diff = jnp.where(tri[:, :, None], b[:, :, :, None, :] - b[:, :, None, :, :], -jnp.inf)
        att = jnp.einsum('bhik,bhjk,bhijk->bhij', qc, kc, jnp.exp(diff))
        o = jnp.einsum('bhij,bhjv->bhiv', att, vc) + jnp.einsum('bhik,bhkv->bhiv', qc * jnp.exp(b), state)
        b_last = b[:, :, -1:, :]
        state = state * jnp.exp(b_last[:, :, 0, :, None]) + jnp.einsum('bhjk,bhjv->bhkv', kc * jnp.exp(b_last - b), vc)
        return state, o

    state0 = jnp.zeros((B, H, K, V), jnp.float32)
    _, o = lax.scan(step, state0, (blocks(q), blocks(k), blocks(v), blocks(log_a)))
    return o.transpose(1, 0, 3, 2, 4).reshape(B, S, H, V)


def gla_branch(q, k, v, g, af, ab, a_up_f, a_bias_f, a_up_b, a_bias_b, norm_g):
    B, S, _ = q.shape
    qh = q.reshape(B, S, GLA_HEADS, GLA_DK) * (GLA_DK ** -0.5)
    kh = k.reshape(B, S, GLA_HEADS, GLA_DK)
    vh = v.reshape(B, S, GLA_HEADS, GLA_DV)
    log_f = (jax.nn.log_sigmoid((af @ a_up_f + a_bias_f).astype(jnp.float32)) / GLA_TAU).reshape(B, S, GLA_HEADS, GLA_DK)
    log_b = (jax.nn.log_sigmoid((ab @ a_up_b + a_bias_b).astype(jnp.float32)) / GLA_TAU).reshape(B, S, GLA_HEADS, GLA_DK)
    o = gla_chunked(qh, kh, vh, log_f) + _rev(gla_chunked(_rev(qh), _rev(kh), _rev(vh), _rev(log_b)))
    o = o * lax.rsqrt(jnp.mean(jnp.square(o), -1, keepdims=True) + LN_EPS)
    o = o.reshape(B, S, GLA_VAL) * norm_g.astype(jnp.float32)
    return (o * jax.nn.silu(g.astype(jnp.float32))).astype(q.dtype)


def rwkv7_scan(r, w, k, v, a, b):
    Bsz, S, H, N = r.shape

    def step(st, inp):
        r_t, w_t, k_t, v_t, a_t, b_t = inp
        sa = jnp.einsum('bhvk,bhk->bhv', st, a_t)
        st = st * w_t[:, :, None, :] + sa[..., None] * b_t[:, :, None, :] + v_t[..., None] * k_t[:, :, None, :]
        return st, jnp.einsum('bhvk,bhk->bhv', st, r_t)

    xs = tuple(jnp.moveaxis(t.astype(jnp.float32), 1, 0) for t in (r, w, k, v, a, b))
    _, y = lax.scan(step, jnp.zeros((Bsz, H, N, N), jnp.float32), xs)
    return jnp.moveaxis(y, 0, 1)


def rwkv_branch(p, mu, w0_f, w_up_f, w0_b, w_up_b, a0, a_up, g_up, k_k, k_a, r_k, ln_g, ln_b):
    B, S, _ = p.shape
    prev = jnp.pad(p, ((0, 0), (1, 0), (0, 0)))[:, :-1]
    nxt = jnp.pad(p, ((0, 0), (0, 1), (0, 0)))[:, 1:]
    p = p + mu * (0.5 * (prev + nxt) - p)
    r, k, v, xwf, xwb, xa, xg = _split(p, RW_SPLITS)
    f32 = jnp.float32
    w_f = jnp.exp(-RW_DECAY_SCALE * jax.nn.sigmoid((w0_f + jnp.tanh(xwf) @ w_up_f).astype(f32)))
    w_b = jnp.exp(-RW_DECAY_SCALE * jax.nn.sigmoid((w0_b + jnp.tanh(xwb) @ w_up_b).astype(f32)))
    a = jax.nn.sigmoid((a0 + xa @ a_up).astype(f32))
    g = (jax.nn.sigmoid(xg) @ g_up).astype(f32)
    heads = lambda t: t.astype(f32).reshape(B, S, RW_HEADS, RW_HEAD)
    kk = heads(k * k_k)
    kk = kk / jnp.maximum(jnp.sqrt(jnp.sum(jnp.square(kk), -1, keepdims=True)), 1e-12)
    k = k.astype(f32) * (1.0 + (a - 1.0) * k_a.astype(f32))
    rh, kh, vh, ah = heads(r), heads(k), heads(v), heads(a)
    wfh, wbh = heads(w_f), heads(w_b)
    a_vec, b_vec = -kk, kk * ah
    y = rwkv7_scan(rh, wfh, kh, vh, a_vec, b_vec) + _rev(rwkv7_scan(_rev(rh), _rev(wbh), _rev(kh), _rev(vh), _rev(a_vec), _rev(b_vec)))
    mu_y = jnp.mean(y, -1, keepdims=True)
    var_y = jnp.mean(jnp.square(y - mu_y), -1, keepdims=True)
    y = (y - mu_y) * lax.rsqrt(var_y + RW_LN_EPS)
    y = y.reshape(B, S, RW_WIDTH) * ln_g.astype(f32) + ln_b.astype(f32)
    bonus = jnp.sum(rh * kh * r_k.astype(f32), -1, keepdims=True) * vh
    y = (y + bonus.reshape(B, S, RW_WIDTH)) * g
    return y.astype(p.dtype)


def expert_choice_ffn(h, w_router, w1, w3, w2):
    B, S, D = h.shape
    cap = CAPACITY * S // N_EXPERTS
    aff = jax.nn.softmax(jnp.einsum('bsd,de->bse', h, w_router).astype(jnp.float32), axis=-1)
    gate, idx = lax.top_k(jnp.swapaxes(aff, 1, 2), cap)
    xe = jax.vmap(lambda hb, ib: hb[ib])(h, idx)
    hid = jax.nn.silu(jnp.einsum('becd,edf->becf', xe, w1)) * jnp.einsum('becd,edf->becf', xe, w3)
    eo = jnp.einsum('becf,efd->becd', hid, w2) * gate[..., None].astype(h.dtype)
    scatter = lambda ib, vb: jnp.zeros((S, D), eo.dtype).at[ib.reshape(-1)].add(vb.reshape(-1, D))
    return jax.vmap(scatter)(idx, eo)


def hybrid_layer(x, w_in, gla_a_up_f, gla_a_bias_f, gla_a_up_b, gla_a_bias_b, gla_norm_g,
                 rw_mu, rw_w0_f, rw_w_up_f, rw_w0_b, rw_w_up_b, rw_a0, rw_a_up, rw_g_up,
                 rw_k_k, rw_k_a, rw_r_k, rw_ln_g, rw_ln_b, w_up_gla, w_up_rwkv, w_out,
                 ln1_g, ln1_b, w_router, w1, w3, w2, ln2_g, ln2_b):
    cols = jnp.einsum('bsd,dc->bsc', x, w_in)
    gla_p, rw_p, gate_p = _split(cols, (GLA_COLS, RW_COLS, 2 * D_MODEL))
    q, k, v, g, af, ab = _split(gla_p, GLA_SPLITS)
    y_gla = gla_branch(q, k, v, g, af, ab, gla_a_up_f, gla_a_bias_f, gla_a_up_b, gla_a_bias_b, gla_norm_g)
    y_rw = rwkv_branch(rw_p, rw_mu, rw_w0_f, rw_w_up_f, rw_w0_b, rw_w_up_b, rw_a0, rw_a_up, rw_g_up,
                       rw_k_k, rw_k_a, rw_r_k, rw_ln_g, rw_ln_b)
    gate_gla, gate_rw = _split(gate_p, (D_MODEL, D_MODEL))
    merged = jax.nn.sigmoid(gate_gla) * (y_gla @ w_up_gla) + jax.nn.sigmoid(gate_rw) * (y_rw @ w_up_rwkv)
    x = layer_norm(DEEPNORM_ALPHA * x + merged @ w_out, ln1_g, ln1_b)
    x = layer_norm(DEEPNORM_ALPHA * x + expert_choice_ffn(x, w_router, w1, w3, w2), ln2_g, ln2_b)
    return x


def setup_inputs(seed: int = 0) -> dict:
    key = jax.random.key(seed)
    ks = jax.random.split(key, 40)
    f32 = jnp.float32
    L, D, E, F = DEPTH, D_MODEL, N_EXPERTS, D_FF_EXPERT
    nrm = lambda i, shape, scale: jax.random.normal(ks[i], shape, f32) * scale
    return {
        'x': nrm(0, (BATCH, SEQ, D), 1.0),
        'w_in': nrm(1, (L, D, N_IN_COLS), D ** -0.5),
        'gla_a_up_f': nrm(2, (L, GLA_LR, GLA_KEY), GLA_LR ** -0.5),
        'gla_a_bias_f': nrm(3, (L, GLA_KEY), 0.5) + 1.0,
        'gla_a_up_b': nrm(4, (L, GLA_LR, GLA_KEY), GLA_LR ** -0.5),
        'gla_a_bias_b': nrm(5, (L, GLA_KEY), 0.5) + 1.0,
        'gla_norm_g': 1.0 + nrm(6, (L, GLA_VAL), 0.02),
        'rw_mu': jax.random.uniform(ks[7], (L, RW_COLS), f32, 0.0, 1.0),
        'rw_w0_f': nrm(8, (L, RW_WIDTH), 1.0) - 0.5,
        'rw_w_up_f': nrm(9, (L, RW_W_LR, RW_WIDTH), 0.5 * RW_W_LR ** -0.5),
        'rw_w0_b': nrm(10, (L, RW_WIDTH), 1.0) - 0.5,
        'rw_w_up_b': nrm(11, (L, RW_W_LR, RW_WIDTH), 0.5 * RW_W_LR ** -0.5),
        'rw_a0': nrm(12, (L, RW_WIDTH), 0.1),
        'rw_a_up': nrm(13, (L, RW_A_LR, RW_WIDTH), 0.5 * RW_A_LR ** -0.5),
        'rw_g_up': nrm(14, (L, RW_G_LR, RW_WIDTH), RW_G_LR ** -0.5),
        'rw_k_k': 0.85 + nrm(15, (L, RW_WIDTH), 0.02),
        'rw_k_a': 1.0 + nrm(16, (L, RW_WIDTH), 0.02),
        'rw_r_k': nrm(17, (L, RW_HEADS, RW_HEAD), 0.1),
        'rw_ln_g': 1.0 + nrm(18, (L, RW_WIDTH), 0.02),
        'rw_ln_b': nrm(19, (L, RW_WIDTH), 0.02),
        'w_up_gla': nrm(20, (L, GLA_VAL, D), GLA_VAL ** -0.5),
        'w_up_rwkv': nrm(21, (L, RW_WIDTH, D), RW_WIDTH ** -0.5),
        'w_out': nrm(22, (L, D, D), DEEPNORM_BETA * D ** -0.5),
        'ln1_g': 1.0 + nrm(23, (L, D), 0.02),
        'ln1_b': nrm(24, (L, D), 0.02),
        'w_router': nrm(25, (L, D, E), D ** -0.5),
        'w1': nrm(26, (L, E, D, F), D ** -0.5),
        'w3': nrm(27, (L, E, D, F), D ** -0.5),
        'w2': nrm(28, (L, E, F, D), DEEPNORM_BETA * F ** -0.5),
        'ln2_g': 1.0 + nrm(29, (L, D), 0.02),
        'ln2_b': nrm(30, (L, D), 0.02),
    }


def reference(x, w_in, gla_a_up_f, gla_a_bias_f, gla_a_up_b, gla_a_bias_b, gla_norm_g,
              rw_mu, rw_w0_f, rw_w_up_f, rw_w0_b, rw_w_up_b, rw_a0, rw_a_up, rw_g_up,
              rw_k_k, rw_k_a, rw_r_k, rw_ln_g, rw_ln_b, w_up_gla, w_up_rwkv, w_out,
              ln1_g, ln1_b, w_router, w1, w3, w2, ln2_g, ln2_b):
    h = x
    for l in range(DEPTH):
        h = hybrid_layer(h, w_in[l], gla_a_up_f[l], gla_a_bias_f[l], gla_a_up_b[l], gla_a_bias_b[l], gla_norm_g[l],
                         rw_mu[l], rw_w0_f[l], rw_w_up_f[l], rw_w0_b[l], rw_w_up_b[l], rw_a0[l], rw_a_up[l], rw_g_up[l],
                         rw_k_k[l], rw_k_a[l], rw_r_k[l], rw_ln_g[l], rw_ln_b[l], w_up_gla[l], w_up_rwkv[l], w_out[l],
                         ln1_g[l], ln1_b[l], w_router[l], w1[l], w3[l], w2[l], ln2_g[l], ln2_b[l])
    return h
```

```python
import numpy as np
import concourse.bass as bass
import concourse.mybir as mybir
from concourse.bass_utils import run_bass_kernel_spmd
from contextlib import ExitStack

F32 = mybir.dt.float32
BF16 = mybir.dt.bfloat16
I32 = mybir.dt.int32
AF = mybir.ActivationFunctionType
ALU = mybir.AluOpType
AX = mybir.AxisListType

ENGS = ("tensor", "vector", "scalar", "gpsimd", "sync")
EPOCH = 16000
T = 4096
D = 2048


class Trk:
    def __init__(self, name, ap=None):
        self.name = name
        self.last_w = []
        self.readers = []
        self.sem = None
        self.dma_cnt = 0
        self.gen = 0
        self.ap = ap


class Prog:
    def __init__(self, nc, es):
        self.nc = nc
        self.es = es
        self.cnt = {e: 0 for e in ENGS}
        self.waited = {e: {} for e in ENGS}
        self.esems = {e: [] for e in ENGS}
        self.nsem = 0
        self.dtrks = []
        self.free_sems = []

    def new_sem(self, name):
        self.nsem += 1
        return self.es.enter_context(self.nc.semaphore(f"{name}_{self.nsem}"))

    def eng_sem(self, e, k):
        ep = (k - 1) // EPOCH
        while len(self.esems[e]) <= ep:
            self.esems[e].append(self.new_sem(f"e_{e}"))
        return self.esems[e][ep], k - ep * EPOCH

    def _tok_wait(self, tok):
        kind, obj, val = tok
        if kind == "eng":
            sem, v = self.eng_sem(obj, val)
            return ("eng", obj), val, sem, v
        return ("sem", id(obj.sem)), val, obj.sem, 16 * val

    def op(self, eng, fn, reads=(), writes=(), dma=False):
        deps = []
        for r in reads:
            deps.extend(r.last_w)
        dst = writes[0] if (dma and writes) else None
        for w in writes:
            par_dma = (dma and w is dst and not w.readers and w.last_w
                       and all(t[0] == "dma" and t[1] is w and t[3] == w.gen for t in w.last_w))
            if not par_dma:
                deps.extend(w.last_w)
            deps.extend(w.readers)
        wd = self.waited[eng]
        engine = getattr(self.nc, eng)
        need = {}
        for tok in deps:
            if tok[0] == "eng" and tok[1] == eng and eng == "tensor":
                continue
            if tok[0] == "dma" and tok[3] != tok[1].gen:
                continue
            key, val, sem, v = self._tok_wait(tok[:3])
            if wd.get(key, 0) >= val:
                continue
            wd[key] = val
            skey = id(sem)
            if skey not in need or need[skey][1] < v:
                need[skey] = (sem, v)
        waits = list(need.values())
        for sem, v in waits[:-1]:
            engine.wait_ge(sem, v)
        if dma:
            if dst.sem is None:
                if self.free_sems:
                    dst.sem, dst.dma_cnt = self.free_sems.pop()
                else:
                    dst.sem, dst.dma_cnt = self.new_sem("d"), 0
                self.dtrks.append(dst)
            dst.dma_cnt += 1
            tok = ("dma", dst, dst.dma_cnt, dst.gen)
            inc = (dst.sem, 16)
        else:
            self.cnt[eng] += 1
            k = self.cnt[eng]
            tok = ("eng", eng, k)
            sem, _ = self.eng_sem(eng, k)
            inc = (sem, 1)
        for w in writes:
            w.last_w = [tok]
            w.readers = []
        for r in reads:
            if r not in writes:
                r.readers.append(tok)
                if len(r.readers) > 48:
                    r.readers = self._prune(r.readers)
        ins = fn(engine)
        if waits:
            ins._wait_ge(waits[-1][0], waits[-1][1])
        ins.then_inc(inc[0], inc[1])
        return tok

    def _prune(self, toks):
        best = {}
        for t in toks:
            if t[0] == "dma" and t[3] != t[1].gen:
                continue
            key = (t[0], t[1] if t[0] == "eng" else id(t[1]))
            if key not in best or best[key][2] < t[2]:
                best[key] = t
        return list(best.values())

    def barrier(self):
        for e in ENGS:
            engine = getattr(self.nc, e)
            wd = self.waited[e]
            for f in ENGS:
                k = self.cnt[f]
                if k == 0 or wd.get(("eng", f), 0) >= k:
                    continue
                if f == e and e == "tensor":
                    continue
                wd[("eng", f)] = k
                sem, v = self.eng_sem(f, k)
                engine.wait_ge(sem, v)
            for t in self.dtrks:
                if t.dma_cnt and wd.get(("sem", id(t.sem)), 0) < t.dma_cnt:
                    wd[("sem", id(t.sem))] = t.dma_cnt
                    engine.wait_ge(t.sem, 16 * t.dma_cnt)
        for t in self.dtrks:
            self.free_sems.append((t.sem, t.dma_cnt))
            t.sem = None
            t.gen += 1
        self.dtrks = []


CM = []
for s in range(0, 3072, 128):
    CM.append((s, 128))
CM.append((3072, 16))
CM.append((3088, 16))
for s in range(3104, 3104 + 3072, 128):
    CM.append((s, 128))
CM += [(6176, 64), (6240, 64), (6304, 64), (6368, 128)]
NSLOT = len(CM)
GATE0 = 6496
NCV = 108
IDXW = 16
NCORE = 2
NEL = 4 if NCORE == 8 else 16
NTI = 8 if NCORE == 8 else 32
CV = dict(gbf=0, gbb=4, gng=8, mu_r=16, mu_k=24, mu_v=32, mu_xwf=40, mu_xwb=41, mu_xa=42, mu_xg=43,
          w0f=44, w0b=52, a0=60, kk=68, ka=76, lng=84, lnb=92, rk=100)


class _Stop(Exception):
    pass


def build(dbg=()):
    try:
        return _build(dbg)
    except _Stop as ex:
        return ex.args[0]


def _build(dbg=()):
    nc = bass.Bass("TRN2", target_bir_lowering=False)
    x = nc.dram_tensor("x", [T, D], F32, kind="ExternalInput")
    w_in = nc.dram_tensor("w_in", [D, 10592], F32, kind="ExternalInput")
    out = nc.dram_tensor("out", [NTI * 128, D], F32, kind="ExternalOutput")
    trow_d = nc.dram_tensor("trow", [128, 2 * NTI], I32, kind="ExternalInput")
    cvec_d = nc.dram_tensor("cvec", [128, NCV], F32, kind="ExternalInput")
    a_up_f_d = nc.dram_tensor("gla_a_up_f", [16, 512], F32, kind="ExternalInput")
    a_up_b_d = nc.dram_tensor("gla_a_up_b", [16, 512], F32, kind="ExternalInput")
    og_d = nc.dram_tensor("og_d", [1024, T], F32)
    w_up_gla_d = nc.dram_tensor("w_up_gla", [1024, D], F32, kind="ExternalInput")
    w_up_rw_d = nc.dram_tensor("w_up_rwkv", [1024, D], F32, kind="ExternalInput")
    w_out_d = nc.dram_tensor("w_out", [D, D], F32, kind="ExternalInput")
    w_router_d = nc.dram_tensor("w_router", [D, 16], F32, kind="ExternalInput")
    ln_d = nc.dram_tensor("ln", [4, D], F32, kind="ExternalInput")
    w1_d = nc.dram_tensor("w1", [NEL, D, 1024], F32, kind="ExternalInput")
    w3_d = nc.dram_tensor("w3", [NEL, D, 1024], F32, kind="ExternalInput")
    w2_d = nc.dram_tensor("w2", [NEL, 1024, D], F32, kind="ExternalInput")
    erow_d = nc.dram_tensor("erow", [128, NEL * 4], I32, kind="ExternalInput")
    eoh_d = nc.dram_tensor("eoh", [128, NEL, 16], F32, kind="ExternalInput")
    boff_d = nc.dram_tensor("boff", [128, 32], F32, kind="ExternalInput")
    mT_d = nc.dram_tensor("mT_d", [D, T], BF16)
    x1_d = nc.dram_tensor("x1_d", [T, D], F32)
    aff_d = nc.dram_tensor("aff_d", [T, 16], F32)
    idx_d = nc.dram_tensor("idx_d", [8192 + 256, IDXW], I32)
    moe_q = [nc.dram_tensor(f"moe_q{i}", [2 * T, 512], F32) for i in range(4)]
    moe_r = [nc.dram_tensor(f"moe_r{i}", [2 * T, 512], F32) for i in range(4)]
    w_up_f_d = nc.dram_tensor("rw_w_up_f", [64, 1024], F32, kind="ExternalInput")
    w_up_b_d = nc.dram_tensor("rw_w_up_b", [64, 1024], F32, kind="ExternalInput")
    a_up_d = nc.dram_tensor("rw_a_up", [64, 1024], F32, kind="ExternalInput")
    g_up_d = nc.dram_tensor("rw_g_up", [128, 1024], F32, kind="ExternalInput")
    rw_d = nc.dram_tensor("rw_d", [9, 1024, T], F32)
    ytm_d = nc.dram_tensor("ytm_d", [2, T, 1024], F32)
    yT_d = nc.dram_tensor("yT_d", [2048, T], BF16)
    xT_d = nc.dram_tensor("xT_d", [D, T], BF16)
    pt_d = nc.dram_tensor("pt_d", [NSLOT * 128, T], F32)
    dbg_out = {}
    if "pt" in dbg:
        dbg_out["pt"] = nc.dram_tensor("dbg_pt", [NSLOT * 128, T], F32, kind="ExternalOutput")
    if "dest" in dbg:
        dbg_out["dest"] = nc.dram_tensor("dbg_dest", [128, 512], I32, kind="ExternalOutput")
        dbg_out["thr"] = nc.dram_tensor("dbg_thr", [16, 8], F32, kind="ExternalOutput")
    if "yt" in dbg:
        dbg_out["yt"] = nc.dram_tensor("dbg_yt", [2048, T], BF16, kind="ExternalOutput")

    es = ExitStack()
    with es:
        P = Prog(nc, es)
        X = Trk("x", x)
        WIN = Trk("w_in", w_in)
        OUT = Trk("out", out)
        XT_D = Trk("xT_d", xT_d)
        PT_D = Trk("pt_d", pt_d)
        OG_D = Trk("og_d", og_d)
        MT_D = Trk("mT_d", mT_d)
        X1_D = Trk("x1_d", x1_d)
        AFF_D = Trk("aff_d", aff_d)
        IDX_D = Trk("idx_d", idx_d)
        MOE_D = Trk("moe_d", None)
        RW_D = Trk("rw_d", rw_d)
        YTM_D = Trk("ytm_d", ytm_d)
        YT_D = Trk("yT_d", yT_d)

        uniq = [0]

        def sb(st, name, shape, dt):
            uniq[0] += 1
            name = f"{name}_u{uniq[0]}"
            return Trk(name, st.enter_context(nc.sbuf_tensor(name, shape, dt)))

        banks = [Trk(f"pb{i}", es.enter_context(nc.psum_tensor(f"pb{i}", [128, 512], F32))) for i in range(8)]
        bank_i = [0]

        only = None
        for f in dbg:
            if f.startswith("only:"):
                only = set(f[5:].split(","))

        def phase(name):
            if only is not None and name not in only:
                return
            with ExitStack() as st_:
                yield st_

        def bank():
            bank_i[0] += 1
            return banks[bank_i[0] % 8]

        ev_i = [0]

        def evac(out_trk, out_ap, pb, in_ap, extra_reads=()):
            ev_i[0] += 1
            if ev_i[0] % 2:
                P.op("vector", lambda e: e.tensor_copy(out=out_ap, in_=in_ap), reads=[pb, *extra_reads], writes=[out_trk])
            else:
                P.op("scalar", lambda e: e.copy(out=out_ap, in_=in_ap), reads=[pb, *extra_reads], writes=[out_trk])

        ident = sb(es, "ident", [128, 128], F32)
        P.op("gpsimd", lambda e: e.memset(ident.ap[:], 1.0), writes=[ident])
        P.op("gpsimd", lambda e: e.affine_select(out=ident.ap[:], in_=ident.ap[:], pattern=[[-1, 128]],
                                                 compare_op=ALU.is_equal, fill=0.0, base=0, channel_multiplier=1),
             reads=[ident], writes=[ident])

        for st in phase("A"):
            xin = [sb(st, f"xin{i}", [128, 4, D], F32) for i in range(2)]
            xtb = [sb(st, f"xtb{i}", [128, 16, 512], BF16) for i in range(2)]
            for bi in range(8):
                xi, xt = xin[bi % 2], xtb[bi % 2]
                P.op("sync", lambda e: e.dma_start(out=xi.ap[:], in_=x[bi * 512:(bi + 1) * 512, :].rearrange("(j p) d -> p j d", p=128)),
                     reads=[X], writes=[xi], dma=True)
                for dt in range(16):
                    pb = bank()
                    for j in range(4):
                        P.op("tensor", lambda e: e.transpose(out=pb.ap[:, j * 128:(j + 1) * 128], in_=xi.ap[:, j, dt * 128:(dt + 1) * 128],
                                                             identity=ident.ap[:]), reads=[xi, ident], writes=[pb])
                    evac(xt, xt.ap[:, dt, :], pb, pb.ap[:, :])
                P.op("sync", lambda e: e.dma_start(out=xT_d.ap().rearrange("(kt p) t -> p kt t", p=128)[:, :, bi * 512:(bi + 1) * 512], in_=xt.ap[:]),
                     reads=[xt], writes=[XT_D], dma=True)
            P.barrier()

        for st in phase("B"):
            wg = [sb(st, f"wg{i}", [128, 16, 1024], BF16) for i in range(2)]
            xtb = [sb(st, f"xtb{i}", [128, 16, 512], BF16) for i in range(2)]
            stg = [sb(st, f"stg{i}", [128, 512], F32) for i in range(4)]
            si = 0
            li = 0
            groups = [list(range(g, min(g + 8, NSLOT))) for g in range(0, NSLOT, 8)]
            for gi, grp in enumerate(groups):
                w = wg[gi % 2]
                off = 0
                offs = {}
                for s in grp:
                    c0, n = CM[s]
                    offs[s] = off
                    P.op("gpsimd", lambda e: e.dma_start(out=w.ap[:, :, off:off + n],
                                                         in_=w_in[:, c0:c0 + n].rearrange("(kt p) c -> p kt c", p=128)),
                         reads=[WIN], writes=[w], dma=True)
                    off += n
                for bi in range(8):
                    xt = xtb[li % 2]
                    li += 1
                    P.op("sync", lambda e: e.dma_start(out=xt.ap[:], in_=xT_d.ap().rearrange("(kt p) t -> p kt t", p=128)[:, :, bi * 512:(bi + 1) * 512]),
                         reads=[XT_D], writes=[xt], dma=True)
                    for s in grp:
                        c0, n = CM[s]
                        o = offs[s]
                        pb = bank()
                        for kt in range(16):
                            P.op("tensor", lambda e: e.matmul(pb.ap[0:n, :], lhsT=w.ap[:, kt, o:o + n], rhs=xt.ap[:, kt, :],
                                                              start=(kt == 0), stop=(kt == 15)), reads=[w, xt], writes=[pb])
                        sg = stg[si % 4]
                        si += 1
                        evac(sg, sg.ap[0:n, :], pb, pb.ap[0:n, :])
                        P.op("sync", lambda e: e.dma_start(out=pt_d[s * 128:s * 128 + n, bi * 512:(bi + 1) * 512], in_=sg.ap[0:n, :]),
                             reads=[sg], writes=[PT_D], dma=True)
            P.barrier()

        cvec = sb(es, "cvec", [128, NCV], F32)
        P.op("sync", lambda e: e.dma_start(out=cvec.ap[:], in_=cvec_d[:, :]), reads=[], writes=[cvec], dma=True)
        ncv = sb(es, "ncv", [128, NCV], F32)
        omc = sb(es, "omc", [128, NCV], F32)
        hcv = sb(es, "hcv", [128, NCV], F32)
        P.op("vector", lambda e: e.tensor_scalar(out=ncv.ap[:], in0=cvec.ap[:], scalar1=-1.0, scalar2=None, op0=ALU.mult), reads=[cvec], writes=[ncv])
        P.op("vector", lambda e: e.tensor_scalar(out=omc.ap[:], in0=cvec.ap[:], scalar1=-1.0, scalar2=1.0, op0=ALU.mult, op1=ALU.add), reads=[cvec], writes=[omc])
        P.op("vector", lambda e: e.tensor_scalar(out=hcv.ap[:], in0=cvec.ap[:], scalar1=0.5, scalar2=None, op0=ALU.mult), reads=[cvec], writes=[hcv])
        ones_f = sb(es, "ones_f", [128, 512], F32)
        P.op("vector", lambda e: e.memset(ones_f.ap[:], 1.0), writes=[ones_f])
        ones_b = sb(es, "ones_b", [128, 128], BF16)
        P.op("vector", lambda e: e.memset(ones_b.ap[:], 1.0), writes=[ones_b])
        maskF = sb(es, "maskF", [128, 128], F32)
        maskB = sb(es, "maskB", [128, 128], F32)
        for mk, sgn in ((maskF, -1), (maskB, 1)):
            P.op("gpsimd", lambda e: e.memset(mk.ap[:], 1.0), writes=[mk])
            P.op("gpsimd", lambda e: e.affine_select(out=mk.ap[:], in_=mk.ap[:], pattern=[[-sgn, 128]], compare_op=ALU.is_ge, fill=0.0,
                                                     base=0, channel_multiplier=sgn), reads=[mk], writes=[mk])

        if "noC" not in dbg:
          for st in phase("C"):
            afab = [sb(st, "af", [16, T], F32), sb(st, "ab", [16, T], F32)]
            aup = [sb(st, "aupf", [16, 512], F32), sb(st, "aupb", [16, 512], F32)]
            for i in range(2):
                P.op("sync", lambda e: e.dma_start(out=afab[i].ap[:], in_=pt_d[(24 + i) * 128:(24 + i) * 128 + 16, :]), reads=[PT_D], writes=[afab[i]], dma=True)
                P.op("sync", lambda e: e.dma_start(out=aup[i].ap[:], in_=(a_up_f_d, a_up_b_d)[i][:, :]), reads=[], writes=[aup[i]], dma=True)
            inb = [[sb(st, f"gin{i}_{j}", [128, 512], F32) for j in range(6)] for i in range(2)]
            lfb = [sb(st, f"lf{i}", [128, 512], F32) for i in range(2)]
            S = sb(st, "S", [128, 256], F32)
            S_bf = sb(st, "S_bf", [128, 256], BF16)
            cum = sb(st, "cum", [128, 128], F32)
            bb = sb(st, "bb", [128, 128], F32)
            e1 = sb(st, "e1", [128, 128], F32)
            e2 = sb(st, "e2", [128, 128], F32)
            e3 = sb(st, "e3", [128, 128], F32)
            qs = sb(st, "qs", [128, 128], BF16)
            ks = sb(st, "ks", [128, 128], BF16)
            kh = sb(st, "kh", [128, 128], F32)
            kv = sb(st, "kv", [128, 384], BF16)
            attT = sb(st, "attT", [128, 128], BF16)
            ostg = [sb(st, f"ostg{i}", [128, 2, 512], F32) for i in range(2)]
            ofw = sb(st, "ofw", [128, 2, 512], F32)
            sq = sb(st, "sq", [128, 2, 512], BF16)
            rstd = sb(st, "rstd", [128, 512], F32)
            sg = sb(st, "sg", [128, 2, 512], F32)
            ybf = sb(st, "ybf", [128, 2, 512], BF16)
            slots = lambda h: [0 + h, 4 + h, 8 + 2 * h, 9 + 2 * h, 16 + 2 * h, 17 + 2 * h]
            it = 0
            for h in range(4):
                for d in range(2):
                    P.op("vector", lambda e: e.memset(S.ap[:], 0.0), writes=[S])
                    P.op("vector", lambda e: e.memset(S_bf.ap[:], 0.0), writes=[S_bf])
                    mk = (maskF, maskB)[d]
                    for bo in range(8):
                        bi = bo if d == 0 else 7 - bo
                        tin = inb[it % 2]
                        lf = lfb[it % 2]
                        og = ostg[it % 2]
                        it += 1
                        for j, sl in enumerate(slots(h)):
                            if j >= 4 and d == 0:
                                continue
                            P.op("sync", lambda e: e.dma_start(out=tin[j].ap[:], in_=pt_d[sl * 128:(sl + 1) * 128, bi * 512:(bi + 1) * 512]),
                                 reads=[PT_D], writes=[tin[j]], dma=True)
                        pb = bank()
                        P.op("tensor", lambda e: e.matmul(pb.ap[:, :], lhsT=aup[d].ap[:, h * 128:(h + 1) * 128], rhs=afab[d].ap[:, bi * 512:(bi + 1) * 512],
                                                          start=True, stop=True), reads=[aup[d], afab[d]], writes=[pb])
                        bcol = ncv.ap[:, (CV["gbf"], CV["gbb"])[d] + h:(CV["gbf"], CV["gbb"])[d] + h + 1]
                        P.op("scalar", lambda e: e.activation(out=lf.ap[:], in_=pb.ap[:, :], func=AF.Exp, bias=bcol, scale=-1.0), reads=[pb, ncv], writes=[lf])
                        P.op("scalar", lambda e: e.activation(out=lf.ap[:], in_=lf.ap[:], func=AF.Ln, bias=1.0), reads=[lf], writes=[lf])
                        P.op("vector", lambda e: e.tensor_scalar(out=lf.ap[:], in0=lf.ap[:], scalar1=-1.0 / 16.0, scalar2=None, op0=ALU.mult), reads=[lf], writes=[lf])
                        for co in range(4):
                            c = co if d == 0 else 3 - co
                            cs = slice(c * 128, (c + 1) * 128)
                            P.op("vector", lambda e: e.tensor_tensor_scan(out=cum.ap[:], data0=ones_f.ap[:, 0:128], data1=lf.ap[:, cs], initial=0.0,
                                                                          op0=ALU.mult, op1=ALU.add), reads=[ones_f, lf], writes=[cum])
                            if d == 0:
                                b_t = cum
                            else:
                                P.op("vector", lambda e: e.tensor_scalar(out=bb.ap[:], in0=cum.ap[:], scalar1=-1.0, scalar2=cum.ap[:, 127:128],
                                                                         op0=ALU.mult, op1=ALU.add), reads=[cum], writes=[bb])
                                P.op("vector", lambda e: e.tensor_tensor(out=bb.ap[:], in0=bb.ap[:], in1=lf.ap[:, cs], op=ALU.add), reads=[bb, lf], writes=[bb])
                                b_t = bb
                            P.op("scalar", lambda e: e.activation(out=e1.ap[:], in_=b_t.ap[:], func=AF.Exp), reads=[b_t], writes=[e1])
                            P.op("scalar", lambda e: e.activation(out=e2.ap[:], in_=b_t.ap[:], func=AF.Exp, scale=-1.0), reads=[b_t], writes=[e2])
                            P.op("scalar", lambda e: e.activation(out=e3.ap[:], in_=b_t.ap[:], func=AF.Exp, scale=-1.0, bias=cum.ap[:, 127:128]), reads=[b_t, cum], writes=[e3])
                            P.op("vector", lambda e: e.scalar_tensor_tensor(out=qs.ap[:], in0=tin[0].ap[:, cs], scalar=128.0 ** -0.5, in1=e1.ap[:],
                                                                            op0=ALU.mult, op1=ALU.mult), reads=[tin[0], e1], writes=[qs])
                            P.op("gpsimd", lambda e: e.tensor_tensor(out=ks.ap[:], in0=tin[1].ap[:, cs], in1=e2.ap[:], op=ALU.mult), reads=[tin[1], e2], writes=[ks])
                            P.op("vector", lambda e: e.tensor_tensor(out=kh.ap[:], in0=tin[1].ap[:, cs], in1=e3.ap[:], op=ALU.mult), reads=[tin[1], e3], writes=[kh])
                            pb1 = bank()
                            P.op("tensor", lambda e: e.transpose(out=pb1.ap[:, 0:128], in_=kh.ap[:], identity=ident.ap[:]), reads=[kh, ident], writes=[pb1])
                            for vt in range(2):
                                P.op("tensor", lambda e: e.transpose(out=pb1.ap[:, 128 + vt * 128:256 + vt * 128], in_=tin[2 + vt].ap[:, cs], identity=ident.ap[:]),
                                     reads=[tin[2 + vt], ident], writes=[pb1])
                            evac(kv, kv.ap[:], pb1, pb1.ap[:, 0:384])
                            pb2 = bank()
                            P.op("tensor", lambda e: e.matmul(pb2.ap[:, 0:128], lhsT=ks.ap[:], rhs=qs.ap[:], start=True, stop=True), reads=[ks, qs], writes=[pb2])
                            P.op("vector", lambda e: e.tensor_tensor(out=attT.ap[:], in0=pb2.ap[:, 0:128], in1=mk.ap[:], op=ALU.mult), reads=[pb2, mk], writes=[attT])
                            pb3 = bank()
                            for vt in range(2):
                                P.op("tensor", lambda e: e.matmul(pb3.ap[:, vt * 128:(vt + 1) * 128], lhsT=kv.ap[:, 128 + vt * 128:256 + vt * 128], rhs=attT.ap[:],
                                                                  start=True, stop=False), reads=[kv, attT], writes=[pb3])
                                P.op("tensor", lambda e: e.matmul(pb3.ap[:, vt * 128:(vt + 1) * 128], lhsT=S_bf.ap[:, vt * 128:(vt + 1) * 128], rhs=qs.ap[:],
                                                                  start=False, stop=True), reads=[S_bf, qs], writes=[pb3])
                            P.op("vector", lambda e: e.tensor_copy(out=og.ap[:, :, cs], in_=pb3.ap[:, 0:256].rearrange("p (v t) -> p v t", v=2)), reads=[pb3], writes=[og])
                            pb4 = bank()
                            P.op("tensor", lambda e: e.matmul(pb4.ap[:, 0:256], lhsT=kv.ap[:, 0:128], rhs=kv.ap[:, 128:384], start=True, stop=True), reads=[kv], writes=[pb4])
                            ecol = e1.ap[:, 127:128] if d == 0 else e1.ap[:, 0:1]
                            P.op("vector", lambda e: e.scalar_tensor_tensor(out=S.ap[:], in0=S.ap[:], scalar=ecol, in1=pb4.ap[:, 0:256], op0=ALU.mult, op1=ALU.add),
                                 reads=[S, e1, pb4], writes=[S])
                            P.op("scalar", lambda e: e.copy(out=S_bf.ap[:], in_=S.ap[:]), reads=[S], writes=[S_bf])
                        ogd = og_d[h * 256:(h + 1) * 256, bi * 512:(bi + 1) * 512].rearrange("(v p) t -> p v t", p=128)
                        if d == 0:
                            P.op("sync", lambda e: e.dma_start(out=ogd, in_=og.ap[:]), reads=[og], writes=[OG_D], dma=True)
                        else:
                            P.op("sync", lambda e: e.dma_start(out=ofw.ap[:], in_=ogd), reads=[OG_D], writes=[ofw], dma=True)
                            P.op("vector", lambda e: e.tensor_tensor(out=ofw.ap[:], in0=ofw.ap[:], in1=og.ap[:], op=ALU.add), reads=[ofw, og], writes=[ofw])
                            P.op("scalar", lambda e: e.activation(out=sq.ap[:], in_=ofw.ap[:], func=AF.Square), reads=[ofw], writes=[sq])
                            pb5 = bank()
                            for vt in range(2):
                                P.op("tensor", lambda e: e.matmul(pb5.ap[:, :], lhsT=ones_b.ap[:], rhs=sq.ap[:, vt, :], start=(vt == 0), stop=(vt == 1)),
                                     reads=[ones_b, sq], writes=[pb5])
                            P.op("scalar", lambda e: e.activation(out=rstd.ap[:], in_=pb5.ap[:, :], func=AF.Sqrt, scale=1.0 / 256.0, bias=1e-5), reads=[pb5], writes=[rstd])
                            P.op("vector", lambda e: e.reciprocal(out=rstd.ap[:], in_=rstd.ap[:]), reads=[rstd], writes=[rstd])
                            for vt in range(2):
                                P.op("scalar", lambda e: e.activation(out=sg.ap[:, vt, :], in_=tin[4 + vt].ap[:], func=AF.Silu), reads=[tin[4 + vt]], writes=[sg])
                                P.op("vector", lambda e: e.tensor_tensor(out=ofw.ap[:, vt, :], in0=ofw.ap[:, vt, :], in1=rstd.ap[:], op=ALU.mult), reads=[ofw, rstd], writes=[ofw])
                                gcol = cvec.ap[:, CV["gng"] + 2 * h + vt:CV["gng"] + 2 * h + vt + 1]
                                P.op("vector", lambda e: e.scalar_tensor_tensor(out=ybf.ap[:, vt, :], in0=ofw.ap[:, vt, :], scalar=gcol, in1=sg.ap[:, vt, :],
                                                                                op0=ALU.mult, op1=ALU.mult), reads=[ofw, cvec, sg], writes=[ybf])
                            P.op("sync", lambda e: e.dma_start(out=yT_d[h * 256:(h + 1) * 256, bi * 512:(bi + 1) * 512].rearrange("(v p) t -> p v t", p=128), in_=ybf.ap[:]),
                                 reads=[ybf], writes=[YT_D], dma=True)
            P.barrier()

        if "noD" not in dbg:
          for st in phase("D1"):
            thf = sb(st, "thf", [64, T], F32)
            thb = sb(st, "thb", [64, T], F32)
            xaT = sb(st, "xaT", [64, T], F32)
            sgx = sb(st, "sgx", [128, T], F32)
            wupf = sb(st, "wupf", [64, 1024], F32)
            wupb = sb(st, "wupb", [64, 1024], F32)
            aupw = sb(st, "aupw", [64, 1024], F32)
            gupw = sb(st, "gupw", [128, 1024], F32)
            for tl, src in ((wupf, w_up_f_d), (wupb, w_up_b_d), (aupw, a_up_d), (gupw, g_up_d)):
                P.op("sync", lambda e: e.dma_start(out=tl.ap[:], in_=src[:, :]), reads=[], writes=[tl], dma=True)
            BD = sb(st, "BD", [128, 128], F32)
            P.op("vector", lambda e: e.memset(BD.ap[:], 0.0), writes=[BD])
            for hh in range(2):
                P.op("vector", lambda e: e.memset(BD.ap[hh * 64:(hh + 1) * 64, hh * 64:(hh + 1) * 64], 1.0), reads=[BD], writes=[BD])
            raws = [sb(st, f"raw{i}", [128, 514], F32) for i in range(4)]
            tmps = [sb(st, f"stmp{i}", [128, 512], F32) for i in range(2)]
            ri = [0]

            def load_shift(slot, nr, mucol, bi, dst, dst_ap):
                rw = raws[ri[0] % 4]
                tp = tmps[ri[0] % 2]
                ri[0] += 1
                lo = bi * 512 - 1
                hi = bi * 512 + 513
                o0 = 0
                if lo < 0:
                    P.op("gpsimd", lambda e: e.memset(rw.ap[0:nr, 0:1], 0.0), writes=[rw])
                    lo, o0 = 0, 1
                if hi > T:
                    P.op("gpsimd", lambda e: e.memset(rw.ap[0:nr, 513:514], 0.0), writes=[rw])
                    hi = T
                P.op("sync", lambda e: e.dma_start(out=rw.ap[0:nr, o0:o0 + hi - lo], in_=pt_d[slot * 128:slot * 128 + nr, lo:hi]), reads=[PT_D, rw], writes=[rw], dma=True)
                P.op("gpsimd", lambda e: e.tensor_tensor(out=tp.ap[0:nr, :], in0=rw.ap[0:nr, 0:512], in1=rw.ap[0:nr, 2:514], op=ALU.add), reads=[rw], writes=[tp])
                P.op("vector", lambda e: e.tensor_scalar(out=tp.ap[0:nr, :], in0=tp.ap[0:nr, :], scalar1=hcv.ap[0:nr, mucol:mucol + 1], scalar2=None, op0=ALU.mult),
                     reads=[tp, hcv], writes=[tp])
                P.op("vector", lambda e: e.scalar_tensor_tensor(out=dst_ap, in0=rw.ap[0:nr, 1:513], scalar=omc.ap[0:nr, mucol:mucol + 1], in1=tp.ap[0:nr, :],
                                                                op0=ALU.mult, op1=ALU.add), reads=[rw, omc, tp], writes=[dst])

            for bi in range(8):
                bs = slice(bi * 512, (bi + 1) * 512)
                for slot, nr, mc, fn, dst in ((50, 64, CV["mu_xwf"], AF.Tanh, thf), (51, 64, CV["mu_xwb"], AF.Tanh, thb),
                                              (52, 64, CV["mu_xa"], None, xaT), (53, 128, CV["mu_xg"], AF.Sigmoid, sgx)):
                    load_shift(slot, nr, mc, bi, dst, dst.ap[0:nr, bs])
                    if fn is not None:
                        P.op("scalar", lambda e: e.activation(out=dst.ap[0:nr, bs], in_=dst.ap[0:nr, bs], func=fn), reads=[dst], writes=[dst])
            rr = [sb(st, f"rr{i}", [128, 512], F32) for i in range(2)]
            kq = [sb(st, f"kq{i}", [128, 512], F32) for i in range(2)]
            vq = [sb(st, f"vq{i}", [128, 512], F32) for i in range(2)]
            wk = [sb(st, f"wk{i}", [128, 512], F32) for i in range(10)]
            it = 0
            for pr in range(8):
                ps_ = slice(pr * 128, (pr + 1) * 128)
                for bi in range(8):
                    bs = slice(bi * 512, (bi + 1) * 512)
                    r_, k_, v_ = rr[it % 2], kq[it % 2], vq[it % 2]
                    it += 1
                    load_shift(26 + pr, 128, CV["mu_r"] + pr, bi, r_, r_.ap[:])
                    load_shift(34 + pr, 128, CV["mu_k"] + pr, bi, k_, k_.ap[:])
                    load_shift(42 + pr, 128, CV["mu_v"] + pr, bi, v_, v_.ap[:])
                    lwf, lwb, a_, g_, kkv, t1, bon, kmod, avec, bvec = wk

                    def store(k, tl):
                        P.op("sync", lambda e: e.dma_start(out=rw_d[k, ps_, bs], in_=tl.ap[:]), reads=[tl], writes=[RW_D], dma=True)
                    for wmat, src, nk, w0c, dstt in ((wupf, thf, 64, CV["w0f"], lwf), (wupb, thb, 64, CV["w0b"], lwb), (aupw, xaT, 64, CV["a0"], a_), (gupw, sgx, 128, None, g_)):
                        pb = bank()
                        P.op("tensor", lambda e: e.matmul(pb.ap[:, :], lhsT=wmat.ap[0:nk, ps_], rhs=src.ap[0:nk, bs], start=True, stop=True), reads=[wmat, src], writes=[pb])
                        if w0c is None:
                            P.op("scalar", lambda e: e.copy(out=dstt.ap[:], in_=pb.ap[:, :]), reads=[pb], writes=[dstt])
                        else:
                            P.op("scalar", lambda e: e.activation(out=dstt.ap[:], in_=pb.ap[:, :], func=AF.Sigmoid, bias=cvec.ap[:, w0c + pr:w0c + pr + 1]), reads=[pb, cvec], writes=[dstt])
                    for tl in (lwf, lwb):
                        P.op("vector", lambda e: e.tensor_scalar(out=tl.ap[:], in0=tl.ap[:], scalar1=-0.606531, scalar2=None, op0=ALU.mult), reads=[tl], writes=[tl])
                    store(5, lwf); store(6, lwb); store(7, g_); store(0, r_); store(2, v_)
                    P.op("vector", lambda e: e.tensor_scalar(out=kkv.ap[:], in0=k_.ap[:], scalar1=cvec.ap[:, CV["kk"] + pr:CV["kk"] + pr + 1], scalar2=None, op0=ALU.mult), reads=[k_, cvec], writes=[kkv])
                    P.op("gpsimd", lambda e: e.tensor_tensor(out=t1.ap[:], in0=kkv.ap[:], in1=kkv.ap[:], op=ALU.mult), reads=[kkv], writes=[t1])
                    pb = bank()
                    P.op("tensor", lambda e: e.matmul(pb.ap[:, :], lhsT=BD.ap[:], rhs=t1.ap[:], start=True, stop=True), reads=[BD, t1], writes=[pb])
                    P.op("scalar", lambda e: e.activation(out=t1.ap[:], in_=pb.ap[:, :], func=AF.Sqrt), reads=[pb], writes=[t1])
                    P.op("vector", lambda e: e.tensor_scalar(out=t1.ap[:], in0=t1.ap[:], scalar1=1e-12, scalar2=None, op0=ALU.max), reads=[t1], writes=[t1])
                    P.op("vector", lambda e: e.reciprocal(out=t1.ap[:], in_=t1.ap[:]), reads=[t1], writes=[t1])
                    P.op("vector", lambda e: e.tensor_tensor(out=kkv.ap[:], in0=kkv.ap[:], in1=t1.ap[:], op=ALU.mult), reads=[kkv, t1], writes=[kkv])
                    P.op("gpsimd", lambda e: e.tensor_scalar(out=avec.ap[:], in0=kkv.ap[:], scalar1=-1.0, scalar2=None, op0=ALU.mult), reads=[kkv], writes=[avec])
                    P.op("vector", lambda e: e.tensor_tensor(out=bvec.ap[:], in0=kkv.ap[:], in1=a_.ap[:], op=ALU.mult), reads=[kkv, a_], writes=[bvec])
                    store(3, avec); store(4, bvec)
                    P.op("vector", lambda e: e.tensor_scalar(out=kmod.ap[:], in0=a_.ap[:], scalar1=cvec.ap[:, CV["ka"] + pr:CV["ka"] + pr + 1],
                                                             scalar2=omc.ap[:, CV["ka"] + pr:CV["ka"] + pr + 1], op0=ALU.mult, op1=ALU.add), reads=[a_, cvec, omc], writes=[kmod])
                    P.op("vector", lambda e: e.tensor_tensor(out=kmod.ap[:], in0=kmod.ap[:], in1=k_.ap[:], op=ALU.mult), reads=[kmod, k_], writes=[kmod])
                    store(1, kmod)
                    P.op("vector", lambda e: e.scalar_tensor_tensor(out=t1.ap[:], in0=r_.ap[:], scalar=cvec.ap[:, CV["rk"] + pr:CV["rk"] + pr + 1], in1=kmod.ap[:],
                                                                    op0=ALU.mult, op1=ALU.mult), reads=[r_, cvec, kmod], writes=[t1])
                    pb = bank()
                    P.op("tensor", lambda e: e.matmul(pb.ap[:, :], lhsT=BD.ap[:], rhs=t1.ap[:], start=True, stop=True), reads=[BD, t1], writes=[pb])
                    P.op("vector", lambda e: e.tensor_tensor(out=bon.ap[:], in0=pb.ap[:, :], in1=v_.ap[:], op=ALU.mult), reads=[pb, v_], writes=[bon])
                    store(8, bon)
            P.barrier()

          for st in phase("D2"):
            def mask4(name, sgn, strict):
                mk = sb(st, name, [128, 4, 128], F32)
                P.op("gpsimd", lambda e: e.memset(mk.ap[:], 1.0), writes=[mk])
                for q in range(4):
                    P.op("gpsimd", lambda e: e.affine_select(out=mk.ap[:, q, :], in_=mk.ap[:, q, :], pattern=[[-sgn, 128]],
                                                             compare_op=(ALU.is_gt if strict else ALU.is_ge), fill=0.0, base=0, channel_multiplier=sgn),
                         reads=[mk], writes=[mk])
                return mk
            mSL = mask4("mSL", 1, True)
            mSU = mask4("mSU", -1, True)
            mLE = mask4("mLE", -1, False)
            mGE = mask4("mGE", 1, False)
            I4 = sb(st, "I4", [128, 4, 128], F32)
            for q in range(4):
                P.op("vector", lambda e: e.tensor_copy(out=I4.ap[:, q, :], in_=ident.ap[:]), reads=[ident], writes=[I4])
            def chain(half_, d_, st):
                ld = [[sb(st, f"ld{i}_{j}", [128, 4, 256], F32) for j in range(6)] for i in range(1)]
                f32t = lambda n, w=64: sb(st, n, [128, 4, w], F32)
                cum, bt, E1, E2, E3, E4, tmpa = [f32t(n) for n in ("cum", "bt", "E1", "E2", "E3", "E4", "tmpa")]
                bd_bf = {n: sb(st, n, [128, 4, 128], BF16) for n in ("AT", "BT", "KT", "RT")}
                bd_f = {n: sb(st, n, [128, 4, 128], F32) for n in ("BH", "KH", "VT")}
                for tl in list(bd_bf.values()) + list(bd_f.values()):
                    P.op("gpsimd", lambda e: e.memset(tl.ap[:], 0.0), writes=[tl])
                Mb = [sb(st, f"M{i}", [128, 4, 128], BF16) for i in range(2)]
                MTb = [sb(st, f"MT{i}", [128, 4, 128], BF16) for i in range(2)]
                TTf = [sb(st, f"TTf{i}", [128, 4, 128], F32) for i in range(2)]
                TTb = [sb(st, f"TTb{i}", [128, 4, 128], BF16) for i in range(2)]
                LakT = sb(st, "LakT", [128, 4, 128], BF16)
                ArbT = sb(st, "ArbT", [128, 4, 128], BF16)
                ArkT = sb(st, "ArkT", [128, 4, 128], BF16)
                BHt = sb(st, "BHt", [128, 4, 128], BF16)
                KHt = sb(st, "KHt", [128, 4, 128], BF16)
                Vst = sb(st, "Vst", [128, 4, 64], BF16)
                RU = sb(st, "RU", [128, 4, 64], BF16)
                Ust = sb(st, "Ust", [128, 4, 64], BF16)
                H = sb(st, "H", [128, 4, 64], F32)
                Hb = sb(st, "Hb", [128, 4, 64], BF16)
                Ystg = [sb(st, f"Ystg{i}", [128, 4, 64], F32) for i in range(2)]
                v4 = lambda pb: pb.ap[:, :].rearrange("p (q c) -> p q c", q=4)
                v4h = lambda pb: pb.ap[:, 0:256].rearrange("p (q c) -> p q c", q=4)
                it = 0
                yi = 0
                for half in (half_,):
                    for d in (d_,):
                        P.op("vector", lambda e: e.memset(H.ap[:], 0.0), writes=[H])
                        P.op("vector", lambda e: e.memset(Hb.ap[:], 0.0), writes=[Hb])
                        m_str_L, m_str_LT = (mSL, mSU) if d == 0 else (mSU, mSL)
                        m_inc_T = mLE if d == 0 else mGE
                        for so in range(16):
                            sc = so if d == 0 else 15 - so
                            L = ld[0]
                            it += 1
                            for j, k in enumerate((0, 1, 2, 3, 4, 5 + d)):
                                P.op("sync", lambda e: e.dma_start(out=L[j].ap[:], in_=rw_d[k, half * 512:(half + 1) * 512, sc * 256:(sc + 1) * 256].rearrange("(q p) t -> p q t", p=128)),
                                     reads=[RW_D], writes=[L[j]], dma=True)
                            Rt, KMt, Vt, AVt, BVt, LWt = L
                            for co in range(4):
                                c = co if d == 0 else 3 - co
                                cs = slice(c * 64, (c + 1) * 64)
                                t0 = sc * 256 + c * 64
                                for q in range(4):
                                    P.op("vector", lambda e: e.tensor_tensor_scan(out=cum.ap[:, q, :], data0=ones_f.ap[:, 0:64], data1=LWt.ap[:, q, cs], initial=0.0,
                                                                                  op0=ALU.mult, op1=ALU.add), reads=[ones_f, LWt], writes=[cum])
                                totb = cum.ap[:, :, 63:64].to_broadcast([128, 4, 64])
                                if d == 0:
                                    b_t = cum
                                else:
                                    P.op("vector", lambda e: e.tensor_tensor(out=bt.ap[:], in0=totb, in1=cum.ap[:], op=ALU.subtract), reads=[cum], writes=[bt])
                                    P.op("vector", lambda e: e.tensor_tensor(out=bt.ap[:], in0=bt.ap[:], in1=LWt.ap[:, :, cs], op=ALU.add), reads=[bt, LWt], writes=[bt])
                                    b_t = bt
                                P.op("scalar", lambda e: e.activation(out=E1.ap[:], in_=b_t.ap[:], func=AF.Exp), reads=[b_t], writes=[E1])
                                P.op("scalar", lambda e: e.activation(out=E2.ap[:], in_=b_t.ap[:], func=AF.Exp, scale=-1.0), reads=[b_t], writes=[E2])
                                P.op("vector", lambda e: e.tensor_tensor(out=tmpa.ap[:], in0=b_t.ap[:], in1=LWt.ap[:, :, cs], op=ALU.subtract), reads=[b_t, LWt], writes=[tmpa])
                                P.op("scalar", lambda e: e.activation(out=E3.ap[:], in_=tmpa.ap[:], func=AF.Exp), reads=[tmpa], writes=[E3])
                                P.op("vector", lambda e: e.tensor_tensor(out=tmpa.ap[:], in0=totb, in1=b_t.ap[:], op=ALU.subtract), reads=[cum, b_t, tmpa], writes=[tmpa])
                                P.op("scalar", lambda e: e.activation(out=E4.ap[:], in_=tmpa.ap[:], func=AF.Exp), reads=[tmpa], writes=[E4])
                                yield
                                k_e = 0
                                for nm, src, Ex in (("AT", AVt, E3), ("BT", BVt, E2), ("KT", KMt, E2), ("RT", Rt, E1), ("BH", BVt, E4), ("KH", KMt, E4), ("VT", Vt, None)):
                                    dstt = bd_bf.get(nm) or bd_f.get(nm)
                                    for hh in range(2):
                                        pp = slice(hh * 64, (hh + 1) * 64)
                                        eng = ("vector", "gpsimd")[k_e % 2]
                                        k_e += 1
                                        if Ex is None:
                                            P.op(eng, lambda e: e.tensor_copy(out=dstt.ap[pp, :, hh * 64:(hh + 1) * 64], in_=src.ap[pp, :, cs]), reads=[src], writes=[dstt])
                                        else:
                                            P.op(eng, lambda e: e.tensor_tensor(out=dstt.ap[pp, :, hh * 64:(hh + 1) * 64], in0=src.ap[pp, :, cs], in1=Ex.ap[pp, :, :], op=ALU.mult),
                                                 reads=[src, Ex], writes=[dstt])
                                AT, BT, KT, RT = bd_bf["AT"], bd_bf["BT"], bd_bf["KT"], bd_bf["RT"]
                                yield

                                def mm4(lhs, rhs, rhs_w=128):
                                    pb = bank()
                                    for q in range(4):
                                        P.op("tensor", lambda e: e.matmul(pb.ap[:, q * rhs_w:(q + 1) * rhs_w], lhsT=lhs.ap[:, q, :], rhs=rhs.ap[:, q, :], start=True, stop=True),
                                             reads=[lhs, rhs], writes=[pb])
                                    return pb
                                for src, dstt in ((bd_f["BH"], BHt), (bd_f["KH"], KHt)):
                                    pb = bank()
                                    for q in range(4):
                                        P.op("tensor", lambda e: e.transpose(out=pb.ap[:, q * 128:(q + 1) * 128], in_=src.ap[:, q, :], identity=ident.ap[:]), reads=[src, ident], writes=[pb])
                                    evac(dstt, dstt.ap[:], pb, v4(pb))
                                pb = bank()
                                for q in range(4):
                                    P.op("tensor", lambda e: e.transpose(out=pb.ap[:, q * 128:(q + 1) * 128], in_=bd_f["VT"].ap[:, q, :], identity=ident.ap[:]), reads=[bd_f["VT"], ident], writes=[pb])
                                for hh in range(2):
                                    pp = slice(hh * 64, (hh + 1) * 64)
                                    P.op("vector", lambda e: e.tensor_copy(out=Vst.ap[pp, :, :], in_=pb.ap[pp, :].rearrange("p (q c) -> p q c", q=4)[:, :, hh * 64:(hh + 1) * 64]), reads=[pb], writes=[Vst])
                                pL = mm4(AT, BT)
                                pLT = mm4(BT, AT)
                                M, MT, TTfc, TTbc = Mb[0], MTb[0], TTf[0], TTb[0]
                                P.op("vector", lambda e: e.tensor_tensor(out=M.ap[:], in0=v4(pL), in1=m_str_L.ap[:], op=ALU.mult), reads=[pL, m_str_L], writes=[M])
                                P.op("vector", lambda e: e.tensor_tensor(out=MT.ap[:], in0=v4(pLT), in1=m_str_LT.ap[:], op=ALU.mult), reads=[pLT, m_str_LT], writes=[MT])
                                P.op("gpsimd", lambda e: e.tensor_tensor(out=TTfc.ap[:], in0=MT.ap[:], in1=I4.ap[:], op=ALU.add), reads=[MT, I4], writes=[TTfc])
                                P.op("gpsimd", lambda e: e.tensor_copy(out=TTbc.ap[:], in_=TTfc.ap[:]), reads=[TTfc], writes=[TTbc])
                                yield
                                for lev in range(1, 6):
                                    Mn, MTn = Mb[lev % 2], MTb[lev % 2]
                                    TTfn, TTbn = TTf[lev % 2], TTb[lev % 2]
                                    p1 = mm4(MT, M)
                                    evac(Mn, Mn.ap[:], p1, v4(p1))
                                    if lev < 5:
                                        p2 = mm4(M, MT)
                                        evac(MTn, MTn.ap[:], p2, v4(p2))
                                    p3 = mm4(Mn, TTbc)
                                    P.op("vector", lambda e: e.tensor_tensor(out=TTfn.ap[:], in0=v4(p3), in1=TTfc.ap[:], op=ALU.add), reads=[p3, TTfc], writes=[TTfn])
                                    P.op("scalar", lambda e: e.copy(out=TTbn.ap[:], in_=TTfn.ap[:]), reads=[TTfn], writes=[TTbn])
                                    M, MT, TTfc, TTbc = Mn, MTn, TTfn, TTbn
                                    yield
                                for lhs, rhs, msk, dstt in ((KT, AT, m_str_LT, LakT), (BT, RT, m_inc_T, ArbT), (KT, RT, m_inc_T, ArkT)):
                                    pq = mm4(lhs, rhs)
                                    P.op("vector", lambda e: e.tensor_tensor(out=dstt.ap[:], in0=v4(pq), in1=msk.ap[:], op=ALU.mult), reads=[pq, msk], writes=[dstt])
                                pb = bank()
                                for q in range(4):
                                    P.op("tensor", lambda e: e.matmul(pb.ap[:, q * 64:(q + 1) * 64], lhsT=AT.ap[:, q, :], rhs=Hb.ap[:, q, :], start=True, stop=False), reads=[AT, Hb], writes=[pb])
                                    P.op("tensor", lambda e: e.matmul(pb.ap[:, q * 64:(q + 1) * 64], lhsT=LakT.ap[:, q, :], rhs=Vst.ap[:, q, :], start=False, stop=True), reads=[LakT, Vst], writes=[pb])
                                evac(RU, RU.ap[:], pb, v4h(pb))
                                yield
                                pb = mm4(TTbc, RU, 64)
                                evac(Ust, Ust.ap[:], pb, v4h(pb))
                                yield
                                pb = bank()
                                for q in range(4):
                                    P.op("tensor", lambda e: e.matmul(pb.ap[:, q * 64:(q + 1) * 64], lhsT=RT.ap[:, q, :], rhs=Hb.ap[:, q, :], start=True, stop=False), reads=[RT, Hb], writes=[pb])
                                    P.op("tensor", lambda e: e.matmul(pb.ap[:, q * 64:(q + 1) * 64], lhsT=ArbT.ap[:, q, :], rhs=Ust.ap[:, q, :], start=False, stop=False), reads=[ArbT, Ust], writes=[pb])
                                    P.op("tensor", lambda e: e.matmul(pb.ap[:, q * 64:(q + 1) * 64], lhsT=ArkT.ap[:, q, :], rhs=Vst.ap[:, q, :], start=False, stop=True), reads=[ArkT, Vst], writes=[pb])
                                ys = Ystg[yi % 2]
                                yi += 1
                                evac(ys, ys.ap[:], pb, v4h(pb))
                                yield
                                for hh in range(2):
                                    P.op("sync", lambda e: e.dma_start(out=ytm_d[d, t0:t0 + 64, half * 512:(half + 1) * 512].rearrange("t (q h v) -> h t q v", h=2, v=64)[hh],
                                                                       in_=ys.ap[hh * 64:(hh + 1) * 64, :, :]), reads=[ys], writes=[YTM_D], dma=True)
                                pb = bank()
                                for q in range(4):
                                    P.op("tensor", lambda e: e.matmul(pb.ap[:, q * 64:(q + 1) * 64], lhsT=BHt.ap[:, q, :], rhs=Ust.ap[:, q, :], start=True, stop=False), reads=[BHt, Ust], writes=[pb])
                                    P.op("tensor", lambda e: e.matmul(pb.ap[:, q * 64:(q + 1) * 64], lhsT=KHt.ap[:, q, :], rhs=Vst.ap[:, q, :], start=False, stop=True), reads=[KHt, Vst], writes=[pb])
                                ecol = (E1.ap[:, :, 63:64] if d == 0 else E1.ap[:, :, 0:1]).to_broadcast([128, 4, 64])
                                P.op("vector", lambda e: e.tensor_tensor(out=H.ap[:], in0=H.ap[:], in1=ecol, op=ALU.mult), reads=[H, E1], writes=[H])
                                P.op("vector", lambda e: e.tensor_tensor(out=H.ap[:], in0=H.ap[:], in1=v4h(pb), op=ALU.add), reads=[H, pb], writes=[H])
                                P.op("scalar", lambda e: e.copy(out=Hb.ap[:], in_=H.ap[:]), reads=[H], writes=[Hb])
                                yield
            for half_ in range(2):
                with ExitStack() as hs:
                    gens = [chain(half_, 0, hs), chain(half_, 1, hs)]
                    while gens:
                        for g_ in list(gens):
                            try:
                                next(g_)
                            except StopIteration:
                                gens.remove(g_)
                    P.barrier()
            P.barrier()

          for st in phase("D3"):
            yf = [sb(st, f"yf{i}", [128, 1024], F32) for i in range(2)]
            yb_ = [sb(st, f"yb{i}", [128, 1024], F32) for i in range(2)]
            st16 = [sb(st, f"st16_{i}", [128, 16], F32) for i in range(3)]
            ysq = sb(st, "ysq", [128, 1024], F32)
            bong = [[sb(st, f"bong{i}_{j}", [128, 8, 512], F32) for j in range(2)] for i in range(2)]
            yo = [sb(st, f"yo{i}", [128, 8, 512], F32) for i in range(2)]
            yob = [sb(st, f"yob{i}", [128, 8, 512], BF16) for i in range(2)]
            for bi in range(8):
                bs = slice(bi * 512, (bi + 1) * 512)
                bg = bong[bi % 2]
                for j, k in enumerate((8, 7)):
                    P.op("sync", lambda e: e.dma_start(out=bg[j].ap[:], in_=rw_d[k, :, bs].rearrange("(q p) t -> p q t", p=128)), reads=[RW_D], writes=[bg[j]], dma=True)
                yo_, yob_ = yo[bi % 2], yob[bi % 2]
                for tt in range(4):
                    t0 = bi * 512 + tt * 128
                    a, b_ = yf[tt % 2], yb_[tt % 2]
                    P.op("sync", lambda e: e.dma_start(out=a.ap[:], in_=ytm_d[0, t0:t0 + 128, :]), reads=[YTM_D], writes=[a], dma=True)
                    P.op("sync", lambda e: e.dma_start(out=b_.ap[:], in_=ytm_d[1, t0:t0 + 128, :]), reads=[YTM_D], writes=[b_], dma=True)
                    P.op("vector", lambda e: e.tensor_tensor(out=a.ap[:], in0=a.ap[:], in1=b_.ap[:], op=ALU.add), reads=[a, b_], writes=[a])
                    a3 = a.ap[:].rearrange("p (h v) -> p h v", v=64)
                    mean, var, rs = st16
                    P.op("vector", lambda e: e.tensor_reduce(out=mean.ap[:], in_=a3, axis=AX.X, op=ALU.add), reads=[a], writes=[mean])
                    P.op("vector", lambda e: e.tensor_scalar(out=mean.ap[:], in0=mean.ap[:], scalar1=1.0 / 64.0, scalar2=None, op0=ALU.mult), reads=[mean], writes=[mean])
                    P.op("vector", lambda e: e.tensor_tensor(out=a3, in0=a3, in1=mean.ap[:].unsqueeze(2).to_broadcast([128, 16, 64]), op=ALU.subtract), reads=[a, mean], writes=[a])
                    P.op("gpsimd", lambda e: e.tensor_tensor(out=ysq.ap[:], in0=a.ap[:], in1=a.ap[:], op=ALU.mult), reads=[a], writes=[ysq])
                    P.op("vector", lambda e: e.tensor_reduce(out=var.ap[:], in_=ysq.ap[:].rearrange("p (h v) -> p h v", v=64), axis=AX.X, op=ALU.add), reads=[ysq], writes=[var])
                    P.op("scalar", lambda e: e.activation(out=rs.ap[:], in_=var.ap[:], func=AF.Sqrt, scale=1.0 / 64.0, bias=64e-5), reads=[var], writes=[rs])
                    P.op("vector", lambda e: e.reciprocal(out=rs.ap[:], in_=rs.ap[:]), reads=[rs], writes=[rs])
                    P.op("vector", lambda e: e.tensor_tensor(out=a3, in0=a3, in1=rs.ap[:].unsqueeze(2).to_broadcast([128, 16, 64]), op=ALU.mult), reads=[a, rs], writes=[a])
                    for g4 in range(2):
                        pb = bank()
                        for q in range(4):
                            pr = g4 * 4 + q
                            P.op("tensor", lambda e: e.transpose(out=pb.ap[:, q * 128:(q + 1) * 128], in_=a.ap[:, pr * 128:(pr + 1) * 128], identity=ident.ap[:]), reads=[a, ident], writes=[pb])
                        for q in range(4):
                            pr = g4 * 4 + q
                            P.op("scalar", lambda e: e.activation(out=yo_.ap[:, pr, tt * 128:(tt + 1) * 128], in_=pb.ap[:, q * 128:(q + 1) * 128], func=AF.Identity,
                                                                  scale=cvec.ap[:, CV["lng"] + pr:CV["lng"] + pr + 1], bias=cvec.ap[:, CV["lnb"] + pr:CV["lnb"] + pr + 1]),
                                 reads=[pb, cvec], writes=[yo_])
                P.op("vector", lambda e: e.tensor_tensor(out=yo_.ap[:], in0=yo_.ap[:], in1=bg[0].ap[:], op=ALU.add), reads=[yo_, bg[0]], writes=[yo_])
                P.op("vector", lambda e: e.tensor_tensor(out=yob_.ap[:], in0=yo_.ap[:], in1=bg[1].ap[:], op=ALU.mult), reads=[yo_, bg[1]], writes=[yob_])
                P.op("sync", lambda e: e.dma_start(out=yT_d[1024:2048, bs].rearrange("(q p) t -> p q t", p=128), in_=yob_.ap[:]), reads=[yob_], writes=[YT_D], dma=True)
            P.barrier()

        if "stopD" in dbg:
            raise _Stop(nc)
        for st in phase("E"):
            wgt = [[sb(st, f"wgt{i}_{j}", [128, 16, 128], BF16) for j in range(2)] for i in range(2)]
            wup = [[sb(st, f"wup{i}_{j}", [128, 8, 128], BF16) for j in range(2)] for i in range(2)]
            xtb = [sb(st, f"extb{i}", [128, 16, 512], BF16) for i in range(2)]
            ytb = [sb(st, f"eytb{i}", [128, 16, 512], BF16) for i in range(2)]
            sgl = [sb(st, f"sgl{i}", [128, 512], F32) for i in range(2)]
            m12 = [sb(st, f"m12_{i}", [128, 512], F32) for i in range(2)]
            mbf = [sb(st, f"mbf{i}", [128, 512], BF16) for i in range(2)]
            li = 0
            for dt in range(16):
                wg_, wu_ = wgt[dt % 2], wup[dt % 2]
                for j in range(2):
                    c0 = GATE0 + j * 2048 + dt * 128
                    P.op("gpsimd", lambda e: e.dma_start(out=wg_[j].ap[:], in_=w_in[:, c0:c0 + 128].rearrange("(kt p) c -> p kt c", p=128)), reads=[WIN], writes=[wg_[j]], dma=True)
                    P.op("gpsimd", lambda e: e.dma_start(out=wu_[j].ap[:], in_=(w_up_gla_d, w_up_rw_d)[j][:, dt * 128:(dt + 1) * 128].rearrange("(kt p) c -> p kt c", p=128)),
                         reads=[], writes=[wu_[j]], dma=True)
                for bi in range(8):
                    bs = slice(bi * 512, (bi + 1) * 512)
                    xt, yt = xtb[li % 2], ytb[li % 2]
                    mb = mbf[li % 2]
                    li += 1
                    P.op("sync", lambda e: e.dma_start(out=xt.ap[:], in_=xT_d.ap().rearrange("(kt p) t -> p kt t", p=128)[:, :, bs]), reads=[XT_D], writes=[xt], dma=True)
                    P.op("sync", lambda e: e.dma_start(out=yt.ap[:], in_=yT_d.ap().rearrange("(kt p) t -> p kt t", p=128)[:, :, bs]), reads=[YT_D], writes=[yt], dma=True)
                    for j in range(2):
                        pg = bank()
                        for kt in range(16):
                            P.op("tensor", lambda e: e.matmul(pg.ap[:, :], lhsT=wg_[j].ap[:, kt, :], rhs=xt.ap[:, kt, :], start=(kt == 0), stop=(kt == 15)), reads=[wg_[j], xt], writes=[pg])
                        P.op("scalar", lambda e: e.activation(out=sgl[j].ap[:], in_=pg.ap[:, :], func=AF.Sigmoid), reads=[pg], writes=[sgl[j]])
                        pu = bank()
                        for kt in range(8):
                            P.op("tensor", lambda e: e.matmul(pu.ap[:, :], lhsT=wu_[j].ap[:, kt, :], rhs=yt.ap[:, j * 8 + kt, :], start=(kt == 0), stop=(kt == 7)), reads=[wu_[j], yt], writes=[pu])
                        P.op("vector", lambda e: e.tensor_tensor(out=m12[j].ap[:], in0=pu.ap[:, :], in1=sgl[j].ap[:], op=ALU.mult), reads=[pu, sgl[j]], writes=[m12[j]])
                    P.op("vector", lambda e: e.tensor_tensor(out=mb.ap[:], in0=m12[0].ap[:], in1=m12[1].ap[:], op=ALU.add), reads=[m12[0], m12[1]], writes=[mb])
                    P.op("sync", lambda e: e.dma_start(out=mT_d[dt * 128:(dt + 1) * 128, bs], in_=mb.ap[:]), reads=[mb], writes=[MT_D], dma=True)
            P.barrier()

        ALPHA = 2.0 ** 0.25
        if "stopE" in dbg:
            raise _Stop(nc)

        def layer_norm_tile(z, tmpt, stat, g_t, b_t):
            P.op("vector", lambda e: e.tensor_reduce(out=stat.ap[:, 0:1], in_=z.ap[:], axis=AX.X, op=ALU.add), reads=[z], writes=[stat])
            P.op("vector", lambda e: e.tensor_scalar(out=stat.ap[:, 0:1], in0=stat.ap[:, 0:1], scalar1=1.0 / D, scalar2=None, op0=ALU.mult), reads=[stat], writes=[stat])
            P.op("vector", lambda e: e.tensor_scalar(out=z.ap[:], in0=z.ap[:], scalar1=stat.ap[:, 0:1], scalar2=None, op0=ALU.subtract), reads=[z, stat], writes=[z])
            P.op("gpsimd", lambda e: e.tensor_tensor(out=tmpt.ap[:], in0=z.ap[:], in1=z.ap[:], op=ALU.mult), reads=[z], writes=[tmpt])
            P.op("vector", lambda e: e.tensor_reduce(out=stat.ap[:, 1:2], in_=tmpt.ap[:], axis=AX.X, op=ALU.add), reads=[tmpt], writes=[stat])
            P.op("scalar", lambda e: e.activation(out=stat.ap[:, 2:3], in_=stat.ap[:, 1:2], func=AF.Sqrt, scale=1.0 / D, bias=1e-5), reads=[stat], writes=[stat])
            P.op("vector", lambda e: e.reciprocal(out=stat.ap[:, 3:4], in_=stat.ap[:, 2:3]), reads=[stat], writes=[stat])
            P.op("vector", lambda e: e.scalar_tensor_tensor(out=z.ap[:], in0=z.ap[:], scalar=stat.ap[:, 3:4], in1=g_t.ap[:], op0=ALU.mult, op1=ALU.mult), reads=[z, stat, g_t], writes=[z])
            P.op("gpsimd", lambda e: e.tensor_tensor(out=z.ap[:], in0=z.ap[:], in1=b_t.ap[:], op=ALU.add), reads=[z, b_t], writes=[z])

        es2 = ExitStack()
        es2.__enter__()
        affT = sb(es2, "affT", [16, T], F32)
        aff_tm = sb(es2, "aff_tm", [128, 32, 16], F32)
        for st in phase("F"):
            wo = sb(st, "wo", [128, 16, D], BF16)
            for kt in range(16):
                P.op("gpsimd", lambda e: e.dma_start(out=wo.ap[:, kt, :], in_=w_out_d[kt * 128:(kt + 1) * 128, :]), reads=[], writes=[wo], dma=True)
            wr = sb(st, "wr", [128, 16, 16], F32)
            P.op("sync", lambda e: e.dma_start(out=wr.ap[:], in_=w_router_d.ap().rearrange("(kt p) e -> p kt e", p=128)), reads=[], writes=[wr], dma=True)
            lng = sb(st, "ln1g", [128, D], F32)
            lnb = sb(st, "ln1b", [128, D], F32)
            P.op("sync", lambda e: e.dma_start(out=lng.ap[:], in_=ln_d[0:1, :].partition_broadcast(128)), reads=[], writes=[lng], dma=True)
            P.op("sync", lambda e: e.dma_start(out=lnb.ap[:], in_=ln_d[1:2, :].partition_broadcast(128)), reads=[], writes=[lnb], dma=True)
            mtt = [sb(st, f"mtt{i}", [128, 16, 128], BF16) for i in range(2)]
            xt_ = [sb(st, f"fx{i}", [128, D], F32) for i in range(2)]
            zt = [sb(st, f"fz{i}", [128, D], F32) for i in range(2)]
            tq = sb(st, "ftmp", [128, D], F32)
            stat = sb(st, "fstat", [128, 4], F32)
            x1T = sb(st, "x1T", [128, 16, 128], F32)
            lg = sb(st, "lg", [128, 16], F32)
            sm = sb(st, "sm", [128, 4], F32)
            for tt in range(32):
                ts_ = slice(tt * 128, (tt + 1) * 128)
                mt, xx, z = mtt[tt % 2], xt_[tt % 2], zt[tt % 2]
                P.op("sync", lambda e: e.dma_start(out=mt.ap[:], in_=mT_d.ap().rearrange("(kt p) t -> p kt t", p=128)[:, :, ts_]), reads=[MT_D], writes=[mt], dma=True)
                P.op("sync", lambda e: e.dma_start(out=xx.ap[:], in_=x[ts_, :]), reads=[X], writes=[xx], dma=True)
                for dq in range(4):
                    pb = bank()
                    for kt in range(16):
                        P.op("tensor", lambda e: e.matmul(pb.ap[:, :], lhsT=mt.ap[:, kt, :], rhs=wo.ap[:, kt, dq * 512:(dq + 1) * 512], start=(kt == 0), stop=(kt == 15)), reads=[mt, wo], writes=[pb])
                    P.op("vector", lambda e: e.scalar_tensor_tensor(out=z.ap[:, dq * 512:(dq + 1) * 512], in0=xx.ap[:, dq * 512:(dq + 1) * 512], scalar=ALPHA, in1=pb.ap[:, :],
                                                                    op0=ALU.mult, op1=ALU.add), reads=[xx, pb], writes=[z])
                layer_norm_tile(z, tq, stat, lng, lnb)
                P.op("sync", lambda e: e.dma_start(out=x1_d[ts_, :], in_=z.ap[:]), reads=[z], writes=[X1_D], dma=True)
                for g4 in range(4):
                    pb = bank()
                    for q in range(4):
                        kt = g4 * 4 + q
                        P.op("tensor", lambda e: e.transpose(out=pb.ap[:, q * 128:(q + 1) * 128], in_=z.ap[:, kt * 128:(kt + 1) * 128], identity=ident.ap[:]), reads=[z, ident], writes=[pb])
                    evac(x1T, x1T.ap[:, g4 * 4:(g4 + 1) * 4, :], pb, pb.ap[:, :].rearrange("p (q c) -> p q c", q=4))
                pb = bank()
                for kt in range(16):
                    P.op("tensor", lambda e: e.matmul(pb.ap[:, 0:16], lhsT=x1T.ap[:, kt, :], rhs=wr.ap[:, kt, :], start=(kt == 0), stop=(kt == 15)), reads=[x1T, wr], writes=[pb])
                P.op("vector", lambda e: e.tensor_copy(out=lg.ap[:], in_=pb.ap[:, 0:16]), reads=[pb], writes=[lg])
                P.op("vector", lambda e: e.tensor_reduce(out=sm.ap[:, 0:1], in_=lg.ap[:], axis=AX.X, op=ALU.max), reads=[lg], writes=[sm])
                P.op("vector", lambda e: e.tensor_scalar(out=sm.ap[:, 1:2], in0=sm.ap[:, 0:1], scalar1=-1.0, scalar2=None, op0=ALU.mult), reads=[sm], writes=[sm])
                P.op("scalar", lambda e: e.activation(out=lg.ap[:], in_=lg.ap[:], func=AF.Exp, bias=sm.ap[:, 1:2]), reads=[lg, sm], writes=[lg])
                P.op("vector", lambda e: e.tensor_reduce(out=sm.ap[:, 2:3], in_=lg.ap[:], axis=AX.X, op=ALU.add), reads=[lg], writes=[sm])
                P.op("vector", lambda e: e.reciprocal(out=sm.ap[:, 3:4], in_=sm.ap[:, 2:3]), reads=[sm], writes=[sm])
                P.op("vector", lambda e: e.tensor_scalar(out=aff_tm.ap[:, tt, :], in0=lg.ap[:], scalar1=sm.ap[:, 3:4], scalar2=None, op0=ALU.mult), reads=[lg, sm], writes=[aff_tm])
                pb = bank()
                P.op("tensor", lambda e: e.transpose(out=pb.ap[0:16, 0:128], in_=aff_tm.ap[:, tt, :], identity=ident.ap[:]), reads=[aff_tm, ident], writes=[pb])
                P.op("vector", lambda e: e.tensor_copy(out=affT.ap[:, ts_], in_=pb.ap[0:16, 0:128]), reads=[pb], writes=[affT])
            P.op("sync", lambda e: e.dma_start(out=aff_d.ap().rearrange("(tt p) e -> p tt e", p=128), in_=aff_tm.ap[:]), reads=[aff_tm], writes=[AFF_D], dma=True)
            P.barrier()

        if "stopF" in dbg:
            raise _Stop(nc)
        for st in phase("G"):
            cmpT = sb(st, "cmpT", [16, T], F32)
            posT = sb(st, "posT", [16, T], F32)
            s16 = sb(st, "s16", [16, 8], F32)
            col = lambda i: s16.ap[:, i:i + 1]
            P.op("vector", lambda e: e.memset(s16.ap[:], 0.0), writes=[s16])
            P.op("vector", lambda e: e.memset(col(1), 1.0), reads=[s16], writes=[s16])
            for itn in range(32):
                P.op("vector", lambda e: e.tensor_tensor(out=col(2), in0=col(0), in1=col(1), op=ALU.add), reads=[s16], writes=[s16])
                P.op("vector", lambda e: e.tensor_scalar(out=col(2), in0=col(2), scalar1=0.5, scalar2=None, op0=ALU.mult), reads=[s16], writes=[s16])
                P.op("vector", lambda e: e.tensor_scalar(out=cmpT.ap[:], in0=affT.ap[:], scalar1=col(2), scalar2=None, op0=ALU.is_ge), reads=[affT, s16], writes=[cmpT])
                P.op("vector", lambda e: e.tensor_reduce(out=col(3), in_=cmpT.ap[:], axis=AX.X, op=ALU.add), reads=[cmpT, s16], writes=[s16])
                P.op("vector", lambda e: e.tensor_scalar(out=col(4), in0=col(3), scalar1=511.5, scalar2=None, op0=ALU.is_ge), reads=[s16], writes=[s16])
                P.op("vector", lambda e: e.tensor_tensor(out=col(5), in0=col(2), in1=col(0), op=ALU.subtract), reads=[s16], writes=[s16])
                P.op("vector", lambda e: e.tensor_tensor(out=col(5), in0=col(5), in1=col(4), op=ALU.mult), reads=[s16], writes=[s16])
                P.op("vector", lambda e: e.tensor_tensor(out=col(6), in0=col(1), in1=col(2), op=ALU.subtract), reads=[s16], writes=[s16])
                P.op("vector", lambda e: e.tensor_scalar(out=col(7), in0=col(4), scalar1=-1.0, scalar2=1.0, op0=ALU.mult, op1=ALU.add), reads=[s16], writes=[s16])
                P.op("vector", lambda e: e.tensor_tensor(out=col(6), in0=col(6), in1=col(7), op=ALU.mult), reads=[s16], writes=[s16])
                P.op("vector", lambda e: e.tensor_tensor(out=col(0), in0=col(0), in1=col(5), op=ALU.add), reads=[s16], writes=[s16])
                P.op("vector", lambda e: e.tensor_tensor(out=col(1), in0=col(1), in1=col(6), op=ALU.subtract), reads=[s16], writes=[s16])
            P.op("vector", lambda e: e.tensor_scalar(out=cmpT.ap[:], in0=affT.ap[:], scalar1=col(0), scalar2=None, op0=ALU.is_ge), reads=[affT, s16], writes=[cmpT])
            ones16 = sb(st, "ones16", [16, T], F32)
            P.op("gpsimd", lambda e: e.memset(ones16.ap[:], 1.0), writes=[ones16])
            P.op("vector", lambda e: e.tensor_tensor_scan(out=posT.ap[:], data0=ones16.ap[:], data1=cmpT.ap[:], initial=0.0, op0=ALU.mult, op1=ALU.add), reads=[ones16, cmpT], writes=[posT])
            ebase = sb(st, "ebase", [16, 1], F32)
            P.op("gpsimd", lambda e: e.iota(ebase.ap[:], pattern=[[0, 1]], base=0, channel_multiplier=512, allow_small_or_imprecise_dtypes=True), writes=[ebase])
            P.op("vector", lambda e: e.tensor_scalar(out=posT.ap[:], in0=posT.ap[:], scalar1=ebase.ap[:, 0:1], scalar2=None, op0=ALU.add), reads=[posT, ebase], writes=[posT])
            P.op("vector", lambda e: e.tensor_tensor(out=posT.ap[:], in0=posT.ap[:], in1=cmpT.ap[:], op=ALU.mult), reads=[posT, cmpT], writes=[posT])
            dumpc = sb(st, "dumpc", [128, 2], F32)
            P.op("gpsimd", lambda e: e.iota(dumpc.ap[:, 0:1], pattern=[[0, 1]], base=8192, channel_multiplier=1, allow_small_or_imprecise_dtypes=True), writes=[dumpc])
            P.op("gpsimd", lambda e: e.iota(dumpc.ap[:, 1:2], pattern=[[0, 1]], base=-8193, channel_multiplier=-1, allow_small_or_imprecise_dtypes=True), reads=[dumpc], writes=[dumpc])
            msk_tm = sb(st, "msk_tm", [128, 32, 16], F32)
            dest_f = sb(st, "dest_f", [128, 32, 16], F32)
            dest_i = sb(st, "dest_i", [128, 32, 16], I32)
            tokid = sb(st, "tokid", [128, 32, IDXW], I32)
            P.op("gpsimd", lambda e: e.iota(tokid.ap[:], pattern=[[128, 32], [0, IDXW]], base=0, channel_multiplier=1), writes=[tokid])
            for tt in range(32):
                pb = bank()
                P.op("tensor", lambda e: e.transpose(out=pb.ap[:, 0:16], in_=posT.ap[:, tt * 128:(tt + 1) * 128], identity=ident.ap[0:16, 0:16]), reads=[posT, ident], writes=[pb])
                P.op("vector", lambda e: e.tensor_copy(out=dest_f.ap[:, tt, :], in_=pb.ap[:, 0:16]), reads=[pb], writes=[dest_f])
            P.op("vector", lambda e: e.tensor_scalar(out=msk_tm.ap[:], in0=dest_f.ap[:], scalar1=0.5, scalar2=None, op0=ALU.is_gt), reads=[dest_f], writes=[msk_tm])
            P.op("vector", lambda e: e.tensor_scalar(out=dest_f.ap[:], in0=dest_f.ap[:], scalar1=dumpc.ap[:, 1:2], scalar2=None, op0=ALU.add), reads=[dest_f, dumpc], writes=[dest_f])
            P.op("vector", lambda e: e.tensor_tensor(out=dest_f.ap[:], in0=dest_f.ap[:], in1=msk_tm.ap[:], op=ALU.mult), reads=[dest_f, msk_tm], writes=[dest_f])
            P.op("vector", lambda e: e.tensor_scalar(out=dest_f.ap[:], in0=dest_f.ap[:], scalar1=dumpc.ap[:, 0:1], scalar2=None, op0=ALU.add), reads=[dest_f, dumpc], writes=[dest_f])
            P.op("vector", lambda e: e.tensor_copy(out=dest_i.ap[:], in_=dest_f.ap[:]), reads=[dest_f], writes=[dest_i])
            if "dest" in dbg:
                DBG3 = Trk("dbgdest", dbg_out["dest"])
                P.op("sync", lambda e: e.dma_start(out=dbg_out["dest"][:, :], in_=dest_i.ap[:].rearrange("p a b -> p (a b)")), reads=[dest_i], writes=[DBG3], dma=True)
                DBG4 = Trk("dbgthr", dbg_out["thr"])
                P.op("sync", lambda e: e.dma_start(out=dbg_out["thr"][:, :], in_=s16.ap[:]), reads=[s16], writes=[DBG4], dma=True)
            for tt in range(32):
                for ex in range(16):
                    if "G_noscatter" in dbg:
                        continue
                    P.op("gpsimd", lambda e: e.indirect_dma_start(out=idx_d[:, :], out_offset=bass.IndirectOffsetOnAxis(ap=dest_i.ap[:, tt, ex:ex + 1], axis=0),
                                                                  in_=tokid.ap[:, tt, :], in_offset=None), reads=[dest_i, tokid], writes=[IDX_D], dma=True)
            P.barrier()
        es2.__exit__(None, None, None)

        if "stopG" in dbg:
            raise _Stop(nc)
        for st in phase("H"):
            zt_ = sb(st, "zeros", [128, 4, 512], F32)
            P.op("vector", lambda e: e.memset(zt_.ap[:], 0.0), writes=[zt_])
            for dq in range(4):
                for a in range(16):
                    P.op("sync", lambda e: e.dma_start(out=moe_q[dq][a * 512:(a + 1) * 512, :].rearrange("(a p) c -> p a c", p=128), in_=zt_.ap[:]), reads=[zt_], writes=[MOE_D], dma=True)
            erow = sb(st, "erow", [128, NEL * 4], I32)
            eoh = sb(st, "eoh", [128, NEL, 16], F32)
            boff = sb(st, "boff", [128, 32], F32)
            idxf = sb(st, "idxf", [128, 4], F32)
            P.op("sync", lambda e: e.dma_start(out=erow.ap[:], in_=erow_d[:, :]), reads=[], writes=[erow], dma=True)
            P.op("sync", lambda e: e.dma_start(out=eoh.ap[:], in_=eoh_d[:, :, :]), reads=[], writes=[eoh], dma=True)
            P.op("sync", lambda e: e.dma_start(out=boff.ap[:], in_=boff_d[:, :]), reads=[], writes=[boff], dma=True)
            w1s = sb(st, "w1s", [128, 16, 1024], BF16)
            w3s = sb(st, "w3s", [128, 16, 1024], BF16)
            w2s = sb(st, "w2s", [128, 8, D], BF16)
            xeT = sb(st, "xeT", [128, 16, 512], BF16)
            hidT = sb(st, "hidT", [128, 8, 512], BF16)
            xe = [sb(st, f"xe{i}", [128, D], F32) for i in range(2)]
            idxs = sb(st, "idxs", [128, 4, IDXW], I32)
            idxb = sb(st, "idxb", [128, 4], I32)
            ga1 = [sb(st, f"ga1_{i}", [128, 16], F32) for i in range(4)]
            gtmp = sb(st, "gtmp", [128, 16], F32)
            gcol = sb(st, "gcol", [128, 4], F32)
            sl = sb(st, "sl", [128, 512], F32)
            eo = [sb(st, f"eo{i}", [128, D], F32) for i in range(2)]
            for ex in range(NEL):
                for kt in range(16):
                    P.op("gpsimd", lambda e: e.dma_start(out=w1s.ap[:, kt, :], in_=w1_d[ex, kt * 128:(kt + 1) * 128, :]), reads=[], writes=[w1s], dma=True)
                    P.op("gpsimd", lambda e: e.dma_start(out=w3s.ap[:, kt, :], in_=w3_d[ex, kt * 128:(kt + 1) * 128, :]), reads=[], writes=[w3s], dma=True)
                for ft in range(8):
                    P.op("gpsimd", lambda e: e.dma_start(out=w2s.ap[:, ft, :], in_=w2_d[ex, ft * 128:(ft + 1) * 128, :]), reads=[], writes=[w2s], dma=True)
                for s4 in range(4):
                    P.op("gpsimd", lambda e: e.indirect_dma_start(out=idxs.ap[:, s4, :], out_offset=None, in_=idx_d[:, :],
                                                                  in_offset=bass.IndirectOffsetOnAxis(ap=erow.ap[:, ex * 4 + s4:ex * 4 + s4 + 1], axis=0)),
                         reads=[IDX_D, erow], writes=[idxs], dma=True)
                P.op("vector", lambda e: e.tensor_scalar(out=idxs.ap[:], in0=idxs.ap[:], scalar1=0, scalar2=T - 1, op0=ALU.max, op1=ALU.min), reads=[idxs], writes=[idxs])
                P.op("vector", lambda e: e.tensor_copy(out=idxf.ap[:], in_=idxs.ap[:, :, 0]), reads=[idxs], writes=[idxf])
                P.op("vector", lambda e: e.tensor_tensor(out=idxf.ap[:], in0=idxf.ap[:], in1=boff.ap[:, 0:4], op=ALU.add), reads=[idxf, boff], writes=[idxf])
                P.op("vector", lambda e: e.tensor_copy(out=idxb.ap[:], in_=idxf.ap[:]), reads=[idxf], writes=[idxb])
                for s4 in range(4):
                    xg = xe[s4 % 2]
                    P.op("gpsimd", lambda e: e.indirect_dma_start(out=xg.ap[:], out_offset=None, in_=x1_d[:, :], in_offset=bass.IndirectOffsetOnAxis(ap=idxs.ap[:, s4, 0:1], axis=0)),
                         reads=[X1_D, idxs], writes=[xg], dma=True)
                    P.op("gpsimd", lambda e: e.indirect_dma_start(out=ga1[s4].ap[:], out_offset=None, in_=aff_d[:, :], in_offset=bass.IndirectOffsetOnAxis(ap=idxs.ap[:, s4, 0:1], axis=0)),
                         reads=[AFF_D, idxs], writes=[ga1[s4]], dma=True)
                    P.op("vector", lambda e: e.tensor_tensor(out=gtmp.ap[:], in0=ga1[s4].ap[:], in1=eoh.ap[:, ex, :], op=ALU.mult), reads=[ga1[s4], eoh], writes=[gtmp])
                    P.op("vector", lambda e: e.tensor_reduce(out=gcol.ap[:, s4:s4 + 1], in_=gtmp.ap[:], axis=AX.X, op=ALU.add), reads=[gtmp, gcol], writes=[gcol])
                    for g4 in range(4):
                        pb = bank()
                        for q in range(4):
                            kt = g4 * 4 + q
                            P.op("tensor", lambda e: e.transpose(out=pb.ap[:, q * 128:(q + 1) * 128], in_=xg.ap[:, kt * 128:(kt + 1) * 128], identity=ident.ap[:]), reads=[xg, ident], writes=[pb])
                        evac(xeT, xeT.ap[:, g4 * 4:(g4 + 1) * 4, s4 * 128:(s4 + 1) * 128], pb, pb.ap[:, :].rearrange("p (q c) -> p q c", q=4))
                for ft in range(8):
                    p1 = bank()
                    for kt in range(16):
                        P.op("tensor", lambda e: e.matmul(p1.ap[:, :], lhsT=w1s.ap[:, kt, ft * 128:(ft + 1) * 128], rhs=xeT.ap[:, kt, :], start=(kt == 0), stop=(kt == 15)), reads=[w1s, xeT], writes=[p1])
                    p3 = bank()
                    for kt in range(16):
                        P.op("tensor", lambda e: e.matmul(p3.ap[:, :], lhsT=w3s.ap[:, kt, ft * 128:(ft + 1) * 128], rhs=xeT.ap[:, kt, :], start=(kt == 0), stop=(kt == 15)), reads=[w3s, xeT], writes=[p3])
                    P.op("scalar", lambda e: e.activation(out=sl.ap[:], in_=p1.ap[:, :], func=AF.Silu), reads=[p1], writes=[sl])
                    P.op("vector", lambda e: e.tensor_tensor(out=hidT.ap[:, ft, :], in0=p3.ap[:, :], in1=sl.ap[:], op=ALU.mult), reads=[p3, sl], writes=[hidT])
                for s4 in range(4):
                    eo_ = eo[s4 % 2]
                    for dq in range(4):
                        pb = bank()
                        for ft in range(8):
                            P.op("tensor", lambda e: e.matmul(pb.ap[:, :], lhsT=hidT.ap[:, ft, s4 * 128:(s4 + 1) * 128], rhs=w2s.ap[:, ft, dq * 512:(dq + 1) * 512], start=(ft == 0), stop=(ft == 7)),
                                 reads=[hidT, w2s], writes=[pb])
                        P.op("vector", lambda e: e.tensor_scalar(out=eo_.ap[:, dq * 512:(dq + 1) * 512], in0=pb.ap[:, :], scalar1=gcol.ap[:, s4:s4 + 1], scalar2=None, op0=ALU.mult),
                             reads=[pb, gcol], writes=[eo_])
                    for dq in range(4):
                        P.op("gpsimd", lambda e: e.indirect_dma_start(out=moe_q[dq][:, :], out_offset=bass.IndirectOffsetOnAxis(ap=idxb.ap[:, s4:s4 + 1], axis=0),
                                                                      in_=eo_.ap[:, dq * 512:(dq + 1) * 512], in_offset=None, compute_op=ALU.add), reads=[eo_, idxb, MOE_D], writes=[MOE_D], dma=True)
            P.barrier()
            if "stopHc" in dbg:
                raise _Stop(nc)
            ccsem = P.new_sem("cc")
            ccscr = sb(st, "ccscr", [128, 4], F32)
            for dq in range(4 if NCORE == 8 else 0):
                nc.gpsimd.collective_compute("AllReduce", ALU.add, replica_groups=[list(range(8))], ins=[moe_q[dq].ap().opt()], outs=[moe_r[dq].ap().opt()]).then_inc(ccsem)
                nc.gpsimd.wait_ge(ccsem, dq + 1)
            P.op("gpsimd", lambda e: e.memset(ccscr.ap[:], 0.0), writes=[ccscr, MOE_D])
            P.barrier()

        for st in phase("I"):
            lng = sb(st, "ln2g", [128, D], F32)
            lnb = sb(st, "ln2b", [128, D], F32)
            P.op("sync", lambda e: e.dma_start(out=lng.ap[:], in_=ln_d[2:3, :].partition_broadcast(128)), reads=[], writes=[lng], dma=True)
            P.op("sync", lambda e: e.dma_start(out=lnb.ap[:], in_=ln_d[3:4, :].partition_broadcast(128)), reads=[], writes=[lnb], dma=True)
            trow = sb(st, "trow", [128, 2 * NTI], I32)
            moe_src = moe_r if NCORE == 8 else moe_q
            P.op("sync", lambda e: e.dma_start(out=trow.ap[:], in_=trow_d[:, :]), reads=[], writes=[trow], dma=True)
            a_ = [sb(st, f"ia{i}", [128, D], F32) for i in range(2)]
            m_ = [sb(st, f"im{i}", [128, D], F32) for i in range(2)]
            tq = sb(st, "itmp", [128, D], F32)
            stat = sb(st, "istat", [128, 4], F32)
            for tt in range(NTI):
                ts_ = slice(tt * 128, (tt + 1) * 128)
                a, m = a_[tt % 2], m_[tt % 2]
                P.op("gpsimd", lambda e: e.indirect_dma_start(out=a.ap[:], out_offset=None, in_=x1_d[:, :],
                                                              in_offset=bass.IndirectOffsetOnAxis(ap=trow.ap[:, tt:tt + 1], axis=0)), reads=[X1_D, trow], writes=[a], dma=True)
                for dq in range(4):
                    P.op("gpsimd", lambda e: e.indirect_dma_start(out=m.ap[:, dq * 512:(dq + 1) * 512], out_offset=None, in_=moe_src[dq][:, :],
                                                                  in_offset=bass.IndirectOffsetOnAxis(ap=trow.ap[:, NTI + tt:NTI + tt + 1], axis=0)), reads=[MOE_D, trow], writes=[m], dma=True)
                P.op("vector", lambda e: e.scalar_tensor_tensor(out=m.ap[:], in0=a.ap[:], scalar=ALPHA, in1=m.ap[:], op0=ALU.mult, op1=ALU.add), reads=[a, m], writes=[m])
                layer_norm_tile(m, tq, stat, lng, lnb)
                P.op("sync", lambda e: e.dma_start(out=out[ts_, :], in_=m.ap[:]), reads=[m], writes=[OUT], dma=True)
            P.barrier()

        if "pt" in dbg:
            DBG = Trk("dbgpt", dbg_out["pt"])
            P.op("sync", lambda e: e.dma_start(out=dbg_out["pt"][:, :], in_=pt_d[:, :]), reads=[PT_D], writes=[DBG], dma=True)
            P.barrier()
        if "yt" in dbg:
            DBG2 = Trk("dbgyt", dbg_out["yt"])
            P.op("sync", lambda e: e.dma_start(out=dbg_out["yt"][:, :], in_=yT_d[:, :]), reads=[YT_D], writes=[DBG2], dma=True)
        P.barrier()
    return nc


def _cols(v, n=128):
    v = np.asarray(v, np.float32).reshape(-1)
    if v.size % 128:
        v = np.concatenate([v, np.zeros(128 - v.size % 128, np.float32)])
    return v.reshape(-1, 128).T


def make_inputs(inp, b, r=0):
    g = lambda k: np.ascontiguousarray(np.asarray(inp[k])[0])
    p = np.arange(128, dtype=np.int32)[:, None]
    js = np.arange(NEL * 4, dtype=np.int32)[None, :]
    erow = np.ascontiguousarray(((NEL * r + js // 4) * 512 + (js % 4) * 128 + p).astype(np.int32))
    eoh = np.zeros((128, NEL, 16), np.float32)
    for j in range(NEL):
        eoh[:, j, NEL * r + j] = 1.0
    bo = b * T if NCORE == 8 else 0
    boff = np.full((128, 32), bo, np.float32)
    tn = np.arange(NTI, dtype=np.int32)[None, :]
    trow = np.ascontiguousarray(np.concatenate([r * NTI * 128 + tn * 128 + p, bo + r * NTI * 128 + tn * 128 + p], axis=1).astype(np.int32))
    mu = g("rw_mu")
    cols = [_cols(g("gla_a_bias_f")), _cols(g("gla_a_bias_b")), _cols(g("gla_norm_g")),
            _cols(mu[0:1024]), _cols(mu[1024:2048]), _cols(mu[2048:3072]), _cols(mu[3072:3136]), _cols(mu[3136:3200]),
            _cols(mu[3200:3264]), _cols(mu[3264:3392]),
            _cols(g("rw_w0_f")), _cols(g("rw_w0_b")), _cols(g("rw_a0")), _cols(g("rw_k_k")), _cols(g("rw_k_a")),
            _cols(g("rw_ln_g")), _cols(g("rw_ln_b")), _cols(g("rw_r_k"))]
    cvec = np.ascontiguousarray(np.concatenate(cols, axis=1))
    assert cvec.shape == (128, NCV), cvec.shape
    return {"x": np.ascontiguousarray(np.asarray(inp["x"])[b]), "w_in": g("w_in"), "cvec": cvec,
            "gla_a_up_f": g("gla_a_up_f"), "gla_a_up_b": g("gla_a_up_b"),
            "w_up_gla": g("w_up_gla"), "w_up_rwkv": g("w_up_rwkv"), "w_out": g("w_out"), "w_router": g("w_router"),
            "ln": np.ascontiguousarray(np.stack([g("ln1_g"), g("ln1_b"), g("ln2_g"), g("ln2_b")])),
            "w1": np.ascontiguousarray(g("w1")[NEL * r:NEL * r + NEL]), "w3": np.ascontiguousarray(g("w3")[NEL * r:NEL * r + NEL]),
            "w2": np.ascontiguousarray(g("w2")[NEL * r:NEL * r + NEL]), "erow": erow, "eoh": eoh, "boff": boff, "trow": trow,
            "rw_w_up_f": g("rw_w_up_f"), "rw_w_up_b": g("rw_w_up_b"), "rw_a_up": g("rw_a_up"), "rw_g_up": g("rw_g_up")}


_NC = [None]


def kernel(**inputs):
    if _NC[0] is None:
        _NC[0] = build()
    nc = _NC[0]
    if NCORE == 8:
        in_maps = [make_inputs(inputs, c // 4, c % 4) for c in range(8)]
        res = run_bass_kernel_spmd(nc, in_maps, core_ids=list(range(8)))
        outs = [np.asarray(res.results[c]["out"], np.float32) for c in range(8)]
        return np.stack([np.concatenate(outs[0:4], axis=0), np.concatenate(outs[4:8], axis=0)])
    in_maps = [make_inputs(inputs, b, 0) for b in range(2)]
    res = run_bass_kernel_spmd(nc, in_maps, core_ids=[0, 1])
    return np.stack([np.asarray(res.results[b]["out"], np.float32) for b in range(2)])
```
